# Optimizing a Trainium2 kernel written in Bass

```python
import jax, jax.numpy as jnp
from jax import lax
import numpy as np

D_MODEL = 1024
BATCH = 2
SEQ = 16384
DEPTH = 4

GRID_W = 64
CTX_LEN = 256
EPS = 1e-6
MLA_HEADS = 8
MLA_NOPE = 64
MLA_ROPE = 32
MLA_V = 64
MLA_Q_RANK = 384
MLA_KV_RANK = 256
MLA_SCALE = (MLA_NOPE + MLA_ROPE) ** -0.5
ROPE_BASE = 10000.0
Q_BLOCK = 128
CM_CHUNK = 128
CM_GROUPS = 4
CM_GROUP_DIM = 128
CM_WIDTH = CM_GROUPS * CM_GROUP_DIM
GLA_HEADS = 4
GLA_DK = 128
GLA_DV = 256
GLA_GATE_RANK = 16
GLA_TAU = 16.0
GLA_CHUNK = 64
D_FF = 4 * D_MODEL

E_Q = MLA_Q_RANK
E_KV = E_Q + MLA_KV_RANK
E_R = E_KV + MLA_ROPE
E_U = E_R + CM_WIDTH
EVEN_IN = E_U + CM_WIDTH
EVEN_MIX = MLA_HEADS * MLA_V + CM_WIDTH
O_K = GLA_HEADS * GLA_DK
O_V = O_K + GLA_HEADS * GLA_DV
O_ZF = O_V + GLA_GATE_RANK
O_ZB = O_ZF + GLA_GATE_RANK
O_Q = O_ZB + GLA_HEADS * GLA_DK
ODD_MIX = GLA_HEADS * GLA_DV
ODD_IN = O_Q + ODD_MIX
N_EVEN = (DEPTH + 1) // 2
N_ODD = DEPTH // 2

kernel_name = "hybrid_mla_chunkmlp_gla_dit"

F32 = jnp.float32


def rmsnorm(x, g):
    xf = x.astype(F32)
    y = xf * lax.rsqrt(jnp.mean(xf * xf, axis=-1, keepdims=True) + EPS)
    return (y * g.astype(F32)).astype(x.dtype)


def modulate(h, shift, scale):
    return h * (1.0 + scale) + shift


def axial_rope_tables(length):
    rows = length // GRID_W
    r = jnp.repeat(jnp.arange(rows, dtype=F32), GRID_W)
    col = jnp.tile(jnp.arange(GRID_W, dtype=F32), rows)
    half = MLA_ROPE // 2
    inv = ROPE_BASE ** (-jnp.arange(0, half, 2, dtype=F32) / half)
    ang_r = r[:, None] * inv
    ang_c = col[:, None] * inv
    ang = jnp.concatenate([ang_r, ang_r, ang_c, ang_c], axis=-1)
    return jnp.cos(ang), jnp.sin(ang)


def apply_axial_rope(x, cos, sin):
    xs = x.reshape(x.shape[:-1] + (2, 2, MLA_ROPE // 4))
    rot = jnp.stack([-xs[..., 1, :], xs[..., 0, :]], axis=-2).reshape(x.shape)
    return (x.astype(F32) * cos + rot.astype(F32) * sin).astype(x.dtype)


def mla_kv(ckv_raw, kr_raw, kv_norm, w_ukv, cos, sin):
    ckv = rmsnorm(ckv_raw, kv_norm)
    kv = (ckv @ w_ukv).reshape(ckv.shape[:-1] + (MLA_HEADS, MLA_NOPE + MLA_V))
    k_rope = kr_raw if cos is None else apply_axial_rope(kr_raw, cos, sin)
    return kv[..., :MLA_NOPE], k_rope, kv[..., MLA_NOPE:]


def mla_q(cq_raw, q_norm, w_uq, cos, sin):
    q = (rmsnorm(cq_raw, q_norm) @ w_uq).reshape(cq_raw.shape[:-1] + (MLA_HEADS, MLA_NOPE + MLA_ROPE))
    q_nope, q_rope = q[..., :MLA_NOPE], q[..., MLA_NOPE:]
    if cos is not None:
        q_rope = apply_axial_rope(q_rope, cos[:, None, :], sin[:, None, :])
    return q_nope, q_rope


def mla_attend(q_nope, q_rope, k_nope, k_rope, v):
    s = (jnp.einsum('bqhd,bkhd->bhqk', q_nope, k_nope, preferred_element_type=F32)
         + jnp.einsum('bqhd,bkd->bhqk', q_rope, k_rope, preferred_element_type=F32)) * MLA_SCALE
    p = jax.nn.softmax(s, axis=-1).astype(v.dtype)
    return jnp.einsum('bhqk,bkhd->bqhd', p, v)


def mla_attend_blocks(q_nope, q_rope, k_nope, k_rope, v):
    B, L = q_nope.shape[:2]
    nb = L // Q_BLOCK
    qn = q_nope.reshape(B, nb, Q_BLOCK, MLA_HEADS, MLA_NOPE).transpose(1, 0, 2, 3, 4)
    qr = q_rope.reshape(B, nb, Q_BLOCK, MLA_HEADS, MLA_ROPE).transpose(1, 0, 2, 3, 4)
    out = lax.map(lambda qs: mla_attend(qs[0], qs[1], k_nope, k_rope, v), (qn, qr))
    return out.transpose(1, 0, 2, 3, 4).reshape(B, L, MLA_HEADS, MLA_V)


def chunk_mlp(u_raw, v_raw, v_norm, ws, bs):
    B, L = u_raw.shape[:2]
    n = L // CM_CHUNK
    u = jax.nn.gelu(u_raw)
    v = jax.nn.gelu(v_raw).reshape(B, n, CM_CHUNK, CM_GROUPS, CM_GROUP_DIM)
    v = rmsnorm(v, v_norm)
    v = jnp.einsum('gts,bnsgc->bntgc', ws, v) + bs.T[:, :, None]
    return u * v.reshape(B, L, CM_WIDTH)


def even_mixer(hl, hc, w_in, q_norm, w_uq, kv_norm, w_ukv, cm_norm, cm_ws, cm_bs, w_out, cos, sin, need_ctx):
    B, L = hl.shape[:2]
    pl = hl @ w_in
    if need_ctx:
        pc = hc @ w_in
        kvc = pc[..., E_Q:E_R]
    else:
        kvc = hc @ w_in[:, E_Q:E_R]
    kc_n, kc_r, vc = mla_kv(kvc[..., :MLA_KV_RANK], kvc[..., MLA_KV_RANK:], kv_norm, w_ukv, None, None)
    kl_n, kl_r, vl = mla_kv(pl[..., E_Q:E_KV], pl[..., E_KV:E_R], kv_norm, w_ukv, cos, sin)
    ql_n, ql_r = mla_q(pl[..., :E_Q], q_norm, w_uq, cos, sin)
    k_n = jnp.concatenate([kl_n, kc_n], axis=1)
    k_r = jnp.concatenate([kl_r, kc_r], axis=1)
    v_all = jnp.concatenate([vl, vc], axis=1)
    al = mla_attend_blocks(ql_n, ql_r, k_n, k_r, v_all)
    ml = chunk_mlp(pl[..., E_R:E_U], pl[..., E_U:], cm_norm, cm_ws, cm_bs)
    yl = jnp.concatenate([al.reshape(B, L, MLA_HEADS * MLA_V), ml], axis=-1) @ w_out
    yc = None
    if need_ctx:
        qc_n, qc_r = mla_q(pc[..., :E_Q], q_norm, w_uq, None, None)
        ac = mla_attend(qc_n, qc_r, kc_n, kc_r, vc)
        mc = chunk_mlp(pc[..., E_R:E_U], pc[..., E_U:], cm_norm, cm_ws, cm_bs)
        Lc = hc.shape[1]
        yc = jnp.concatenate([ac.reshape(B, Lc, MLA_HEADS * MLA_V), mc], axis=-1) @ w_out
    return yl, yc


def gla_log_decay(z, w_up, b):
    g = (z @ w_up + b).astype(F32)
    return (jax.nn.log_sigmoid(g) / GLA_TAU).reshape(z.shape[:2] + (GLA_HEADS, GLA_DK))


def gla_chunked(q, k, v, logd, s0):
    B, L = q.shape[:2]
    n = L // GLA_CHUNK

    def blk(t):
        return t.reshape(B, n, GLA_CHUNK, GLA_HEADS, t.shape[-1]).transpose(1, 0, 3, 2, 4)

    qb, kb, vb, gb = blk(q), blk(k), blk(v), blk(logd)
    bcum = jnp.cumsum(gb, axis=3)
    blast = bcum[..., -1:, :]
    q_t = qb * jnp.exp(bcum)
    k_t = kb * jnp.exp(-bcum)
    k_end = kb * jnp.exp(blast - bcum)
    mask = jnp.tril(jnp.ones((GLA_CHUNK, GLA_CHUNK), dtype=bool))
    a = jnp.where(mask, jnp.einsum('nbhtd,nbhsd->nbhts', q_t, k_t), 0.0)
    o_intra = jnp.einsum('nbhts,nbhsv->nbhtv', a, vb)
    decay = jnp.exp(blast[..., 0, :])

    def step(s, xs):
        qt, ke, vv, dc = xs
        o = jnp.einsum('bhtd,bhdv->bhtv', qt, s)
        s = dc[..., None] * s + jnp.einsum('bhtd,bhtv->bhdv', ke, vv)
        return s, o

    s_fin, o_inter = lax.scan(step, s0, (q_t, k_end, vb, decay))
    o = (o_intra + o_inter).transpose(1, 0, 3, 2, 4).reshape(B, L, GLA_HEADS, GLA_DV)
    return o, s_fin


def gla_final_state(k, v, logd):
    bcum = jnp.cumsum(logd, axis=1)
    kw = k * jnp.exp(bcum[:, -1:] - bcum)
    return jnp.einsum('blhd,blhv->bhdv', kw, v)


def gla_output(o, r_raw, o_norm, w_out, dtype):
    B, L = o.shape[:2]
    o = rmsnorm(o, o_norm).astype(dtype)
    r = jax.nn.silu(r_raw).reshape(B, L, GLA_HEADS, GLA_DV)
    return (o * r).reshape(B, L, ODD_MIX) @ w_out


def odd_mixer(hl, hc, w_in, w_gf, b_gf, w_gb, b_gb, o_norm, w_out, need_ctx):
    flip = lambda t: jnp.flip(t, axis=1)

    def state_inputs(p):
        shp = p.shape[:2]
        k = p[..., :O_K].reshape(shp + (GLA_HEADS, GLA_DK)).astype(F32)
        v = p[..., O_K:O_V].reshape(shp + (GLA_HEADS, GLA_DV)).astype(F32)
        return k, v, gla_log_decay(p[..., O_V:O_ZF], w_gf, b_gf), gla_log_decay(p[..., O_ZF:O_ZB], w_gb, b_gb)

    def query(p):
        return p[..., O_ZB:O_Q].reshape(p.shape[:2] + (GLA_HEADS, GLA_DK)).astype(F32) * (GLA_DK ** -0.5)

    B = hl.shape[0]
    pl = hl @ w_in
    pc = hc @ (w_in if need_ctx else w_in[:, :O_ZB])
    kc, vc, fc, bc = state_inputs(pc)
    yc = None
    if need_ctx:
        qc = query(pc)
        s0 = jnp.zeros((B, GLA_HEADS, GLA_DK, GLA_DV), F32)
        oc_f, s_f = gla_chunked(qc, kc, vc, fc, s0)
        oc_b, s_b = gla_chunked(flip(qc), flip(kc), flip(vc), flip(bc), s0)
        yc = gla_output(oc_f + flip(oc_b), pc[..., O_Q:], o_norm, w_out, hc.dtype)
    else:
        s_f = gla_final_state(kc, vc, fc)
        s_b = gla_final_state(flip(kc), flip(vc), flip(bc))
    kl, vl, fl, bl = state_inputs(pl)
    ql = query(pl)
    ol_f, _ = gla_chunked(ql, kl, vl, fl, s_f)
    ol_b, _ = gla_chunked(flip(ql), flip(kl), flip(vl), flip(bl), s_b)
    yl = gla_output(ol_f + flip(ol_b), pl[..., O_Q:], o_norm, w_out, hl.dtype)
    return yl, yc


def sqrelu_mlp(h, w1, w2):
    return jnp.square(jax.nn.relu(h @ w1)) @ w2


def setup_inputs(seed: int = 0) -> dict:
    key = jax.random.key(seed)
    ks = iter(jax.random.split(key, 40))
    D = D_MODEL

    def nrm(shape, scale):
        return jax.random.normal(next(ks), shape, F32) * scale

    return {
        "x": nrm((BATCH, SEQ, D), 1.0),
        "c": nrm((BATCH, D), 1.0),
        "ctx": nrm((BATCH, CTX_LEN, D), 1.0),
        "c_ctx": nrm((D,), 1.0),
        "ada_w": nrm((DEPTH, D, 6 * D), 0.5 * D ** -0.5),
        "ada_b": nrm((DEPTH, 6 * D), 0.02),
        "norm1_g": 1.0 + nrm((DEPTH, D), 0.02),
        "norm2_g": 1.0 + nrm((DEPTH, D), 0.02),
        "mlp_w1": nrm((DEPTH, D, D_FF), D ** -0.5),
        "mlp_w2": nrm((DEPTH, D_FF, D), D_FF ** -0.5),
        "ev_w_in": nrm((N_EVEN, D, EVEN_IN), D ** -0.5),
        "ev_q_norm": 1.0 + nrm((N_EVEN, MLA_Q_RANK), 0.02),
        "ev_w_uq": nrm((N_EVEN, MLA_Q_RANK, MLA_HEADS * (MLA_NOPE + MLA_ROPE)), MLA_Q_RANK ** -0.5),
        "ev_kv_norm": 1.0 + nrm((N_EVEN, MLA_KV_RANK), 0.02),
        "ev_w_ukv": nrm((N_EVEN, MLA_KV_RANK, MLA_HEADS * (MLA_NOPE + MLA_V)), MLA_KV_RANK ** -0.5),
        "ev_cm_norm": 1.0 + nrm((N_EVEN, CM_GROUP_DIM), 0.02),
        "ev_cm_ws": nrm((N_EVEN, CM_GROUPS, CM_CHUNK, CM_CHUNK), CM_CHUNK ** -0.5),
        "ev_cm_bs": 1.0 + nrm((N_EVEN, CM_GROUPS, CM_CHUNK), 0.02),
        "ev_w_out": nrm((N_EVEN, EVEN_MIX, D), EVEN_MIX ** -0.5),
        "od_w_in": nrm((N_ODD, D, ODD_IN), D ** -0.5),
        "od_w_gf": nrm((N_ODD, GLA_GATE_RANK, GLA_HEADS * GLA_DK), GLA_GATE_RANK ** -0.5),
        "od_b_gf": nrm((N_ODD, GLA_HEADS * GLA_DK), 0.02),
        "od_w_gb": nrm((N_ODD, GLA_GATE_RANK, GLA_HEADS * GLA_DK), GLA_GATE_RANK ** -0.5),
        "od_b_gb": nrm((N_ODD, GLA_HEADS * GLA_DK), 0.02),
        "od_o_norm": 1.0 + nrm((N_ODD, GLA_DV), 0.02),
        "od_w_out": nrm((N_ODD, ODD_MIX, D), ODD_MIX ** -0.5),
        "final_g": 1.0 + nrm((D,), 0.02),
    }


def reference(x, c, ctx, c_ctx, ada_w, ada_b, norm1_g, norm2_g, mlp_w1, mlp_w2,
              ev_w_in, ev_q_norm, ev_w_uq, ev_kv_norm, ev_w_ukv, ev_cm_norm, ev_cm_ws, ev_cm_bs, ev_w_out,
              od_w_in, od_w_gf, od_b_gf, od_w_gb, od_b_gb, od_o_norm, od_w_out, final_g):
    L = x.shape[1]
    cos, sin = axial_rope_tables(L)
    sc = jax.nn.silu(c)
    scc = jax.nn.silu(c_ctx)
    xl, xc = x, ctx
    for i in range(DEPTH):
        need_ctx = i < DEPTH - 1
        ml = [m[:, None, :] for m in jnp.split(sc @ ada_w[i] + ada_b[i], 6, axis=-1)]
        mc = jnp.split(scc @ ada_w[i] + ada_b[i], 6, axis=-1)
        hl = modulate(rmsnorm(xl, norm1_g[i]), ml[0], ml[1])
        hc = modulate(rmsnorm(xc, norm1_g[i]), mc[0], mc[1])
        if i % 2 == 0:
            j = i // 2
            yl, yc = even_mixer(hl, hc, ev_w_in[j], ev_q_norm[j], ev_w_uq[j], ev_kv_norm[j], ev_w_ukv[j],
                                ev_cm_norm[j], ev_cm_ws[j], ev_cm_bs[j], ev_w_out[j], cos, sin, need_ctx)
        else:
            j = i // 2
            yl, yc = odd_mixer(hl, hc, od_w_in[j], od_w_gf[j], od_b_gf[j], od_w_gb[j], od_b_gb[j],
                               od_o_norm[j], od_w_out[j], need_ctx)
        xl = xl + ml[2] * yl
        xl = xl + ml[5] * sqrelu_mlp(modulate(rmsnorm(xl, norm2_g[i]), ml[3], ml[4]), mlp_w1[i], mlp_w2[i])
        if need_ctx:
            xc = xc + mc[2] * yc
            xc = xc + mc[5] * sqrelu_mlp(modulate(rmsnorm(xc, norm2_g[i]), mc[3], mc[4]), mlp_w1[i], mlp_w2[i])
    return rmsnorm(xl, final_g)
```

```python
import numpy as np
import concourse.bass as bass
import concourse.mybir as mybir

F32 = mybir.dt.float32
BF16 = mybir.dt.bfloat16
AF = mybir.ActivationFunctionType
ALU = mybir.AluOpType
AX = mybir.AxisListType

ENGS = ('pe', 'act', 'dve', 'pool', 'sp')
N_DMA_SEMS = 56


class Tok:
    __slots__ = ('w', 'r', 'dsem', 'name')

    def __init__(self, name=''):
        self.w = None
        self.r = []
        self.dsem = None
        self.name = name


class Tile:
    def __init__(self, t, tok=None, name=''):
        self.t = t
        self.tok = tok or Tok(name)

    def __getitem__(self, idx):
        return self.t[idx]


class BankView:
    def __init__(self, h, j):
        self.h, self.j = h, j

    def __getitem__(self, idx):
        if not isinstance(idx, tuple):
            idx = (idx, slice(None))
        return self.h[(idx[0], self.j) + tuple(idx[1:])]


class Sched:
    def __init__(self, nc):
        self.nc = nc
        self.ops = {e: [] for e in ENGS}
        self.sem = {e: nc.alloc_semaphore('s_' + e) for e in ENGS}
        self.cnt = {e: 0 for e in ENGS}
        self.seen = {e: {} for e in ENGS}
        self.dsems = [nc.alloc_semaphore('d%d' % i) for i in range(N_DMA_SEMS)]
        self.dcnt = {s.num: 0 for s in self.dsems}
        self.dfree = list(self.dsems)
        self.swsems = []
        self.semobj = {s.num: s for s in list(self.sem.values()) + self.dsems}
        self.sb_off = None
        self.sb_base = None
        self.uid = 0
        self.banks = []
        self.pairs = [nc.alloc_psum_tensor('psp%d' % i, [128, 2, 512], F32) for i in range(4)]
        for i in range(8):
            self.banks.append(Tile(BankView(self.pairs[i // 2], i % 2), name='ps%d' % i))
        self.bank_rr = 0
        self.rr_n = 8

    def arena(self, base, top):
        self.sb_base, self.sb_top = base, top
        self.sb_off = base

    def mark(self):
        return self.sb_off

    def release(self, mark):
        self.sb_off = mark

    def tile(self, shape, dtype, name=''):
        esz = 2 if dtype == BF16 else 4
        n = 1
        for s in shape[1:]:
            n *= s
        nbytes = (n * esz + 31) // 32 * 32
        off = (self.sb_off + 31) // 32 * 32
        assert off + nbytes <= self.sb_top, 'SBUF overflow %s need %d at %d top %d' % (name, nbytes, off, self.sb_top)
        self.uid += 1
        t = self.nc.alloc_sbuf_tensor_at('%s_%d' % (name or 't', self.uid), list(shape), dtype, offset=off)
        self.sb_off = off + nbytes
        return Tile(t, name=name)

    def psum(self, i=None):
        if i is None:
            i = self.bank_rr
            i = i % self.rr_n
            self.bank_rr = (i + 1) % self.rr_n
        return self.banks[i]

    def _waits(self, eng, R, W):
        need = {}

        def add(c):
            if c is None:
                return
            s, v = c
            if need.get(s, 0) < v:
                need[s] = v
        for t in R:
            add(t.w)
        for t in W:
            add(t.w)
            for c in t.r:
                add(c)
        out = []
        own = self.sem[eng].num
        seen = self.seen[eng]
        for s, v in need.items():
            if s == own and eng == 'pe':
                continue
            if seen.get(s, 0) >= v:
                continue
            seen[s] = v
            out.append((self.semobj[s], v))
        return out

    def _commit(self, comp, R, W):
        for t in R:
            t.r.append(comp)
        for t in W:
            t.w = comp
            t.r = []

    def op(self, eng, fn, R=(), W=()):
        R = [x.tok if isinstance(x, Tile) else x for x in R]
        W = [x.tok if isinstance(x, Tile) else x for x in W]
        waits = self._waits(eng, R, W)
        self.cnt[eng] += 1
        comp = (self.sem[eng].num, self.cnt[eng])
        self.ops[eng].append((fn, waits, (self.sem[eng], 1)))
        self._commit(comp, R, W)
        return comp

    def dma(self, eng, out, in_, R=(), W=(), semtok=None):
        R = [x.tok if isinstance(x, Tile) else x for x in R]
        W = [x.tok if isinstance(x, Tile) else x for x in W]
        st = semtok or (W[0] if W else R[0])
        if isinstance(st, Tile):
            st = st.tok
        if eng == 'pool':
            ns = self.nc.alloc_semaphore('sw%d' % len(self.semobj))
            self.semobj[ns.num] = ns
            self.dcnt[ns.num] = 0
            self.swsems.append(ns)
            st.dsem = ns
        elif st.dsem is None or st.dsem in self.swsems:
            st.dsem = self.dfree.pop()
        waits = self._waits(eng, R, W)
        self.dcnt[st.dsem.num] += 16
        comp = (st.dsem.num, self.dcnt[st.dsem.num])
        self.ops[eng].append((lambda e: e.dma_start(out=out, in_=in_), waits, (st.dsem, 16)))
        self._commit(comp, R, W)
        return comp

    def barrier(self, recycle=True):
        for e in ENGS:
            waits = []
            seen = self.seen[e]
            for o in ENGS:
                s = self.sem[o]
                if self.cnt[o] > seen.get(s.num, 0):
                    seen[s.num] = self.cnt[o]
                    waits.append((s, self.cnt[o]))
            for s in self.dsems + self.swsems:
                v = self.dcnt[s.num]
                if v > seen.get(s.num, 0):
                    seen[s.num] = v
                    waits.append((s, v))
            if waits:
                self.ops[e].append((None, waits, None))
        if recycle:
            self.dfree = list(self.dsems)

    def emit(self):
        nc = self.nc
        ops = self.ops

        def run(name, eng):
            for fn, waits, inc in ops[name]:
                for s, v in waits:
                    eng.wait_ge(s, v)
                if fn is not None:
                    ins = fn(eng)
                    if inc is not None:
                        ins.then_inc(inc[0], inc[1])

        with nc.Block() as block:
            @block.tensor
            def _(e):
                run('pe', e)

            @block.scalar
            def _(e):
                run('act', e)

            @block.vector
            def _(e):
                run('dve', e)

            @block.gpsimd
            def _(e):
                run('pool', e)

            @block.sync
            def _(e):
                run('sp', e)

    def mm(self, out, lhsT, rhs, start, stop, R, W):
        return self.op('pe', lambda e: e.matmul(out, lhsT, rhs, start=start, stop=stop), R, W)

    def act(self, out, in_, func, R, W, bias=None, scale=None, accum_out=None, eng='act'):
        kw = {}
        if bias is not None:
            kw['bias'] = bias
        if scale is not None:
            kw['scale'] = scale
        if accum_out is not None:
            kw['accum_out'] = accum_out
        return self.op('act', lambda e: e.activation(out, in_, func, **kw), R, W)

    def tt(self, eng, out, a, b, op, R, W):
        return self.op(eng, lambda e: e.tensor_tensor(out, a, b, op), R, W)

    def ts(self, eng, out, a, s1, s2, op0, op1, R, W):
        if op1 is None:
            return self.op(eng, lambda e: e.tensor_scalar(out, a, s1, None, op0), R, W)
        return self.op(eng, lambda e: e.tensor_scalar(out, a, s1, s2, op0, op1), R, W)

    def stt(self, eng, out, a, s, b, op0, op1, R, W):
        return self.op(eng, lambda e: e.scalar_tensor_tensor(out, a, s, b, op0, op1), R, W)

    def copy(self, eng, out, in_, R, W):
        if eng == 'act':
            return self.op('act', lambda e: e.copy(out, in_), R, W)
        return self.op(eng, lambda e: e.tensor_copy(out, in_), R, W)

    def memset(self, eng, ap, val, W):
        return self.op(eng, lambda e: e.memset(ap, val), (), W)

    def recip(self, out, in_, R, W):
        return self.op('dve', lambda e: e.reciprocal(out, in_), R, W)

    def allgather(self, out, in_, groups, tok):
        ns = self.nc.alloc_semaphore('cc%d' % len(self.semobj))
        self.semobj[ns.num] = ns
        self.dcnt[ns.num] = 16
        self.swsems.append(ns)
        waits = self._waits('pool', [], [tok])
        comp = (ns.num, 16)
        self.ops['pool'].append((lambda e: e.collective_compute('AllGather', ALU.bypass, groups, [in_], [out]), waits, (ns, 16)))
        self._commit(comp, [], [tok])
        return comp

D = 1024
DFF = 4096
NCTX = 256
EPS = 1e-6
DEPTH = 4
E_Q, E_KV, E_R, E_U, EVEN_IN = 384, 640, 672, 1184, 1696
O_K, O_V, O_ZF, O_ZB, O_Q, ODD_IN = 512, 1536, 1552, 1568, 2080, 3104
MLA_SCALE = 96 ** -0.5
LV = 72
NV = DEPTH * LV + 24
SB_BASE, SB_TOP = 16512, 229344


class Env:
    def __init__(self, nc, fused):
        self.nc, self.fused = nc, fused
        self.t = {}
        self.ins, self.outs = {}, {}

    def r(self, name, shape, dtype):
        if name not in self.t:
            self.t[name] = self.nc.dram_tensor(name, list(shape), dtype, kind='ExternalInput').ap()
            self.ins[name] = (tuple(shape), dtype)
        return self.t[name]

    inp = r

    def w(self, name, shape, dtype, external=False):
        if name not in self.t:
            kind = 'ExternalOutput' if (external or not self.fused) else 'Internal'
            self.t[name] = self.nc.dram_tensor(name, list(shape), dtype, kind=kind).ap()
            if kind == 'ExternalOutput':
                self.outs[name] = (tuple(shape), dtype)
        return self.t[name]


class B:
    def __init__(self, nc, Lc, fused, G=4):
        self.nc = nc
        self.G = G
        self.Lc = Lc
        self.NT = Lc + NCTX
        self.env = Env(nc, fused)
        self.S = S = Sched(nc)
        S.arena(SB_BASE, SB_TOP)
        self.fused = fused
        self.vecs = S.tile([128, NV], F32, 'vecs')
        self.ones = S.tile([128, 128], BF16, 'ones')
        self.mod = S.tile([128, 48, 2], F32, 'mod')
        self.ab = S.tile([128, 4, 8, 2], F32, 'ab')
        self.sc = S.tile([128, 8, 2], F32, 'sc')
        S.memset('pool', self.ones[:], 1.0, W=[self.ones])
        v = self.env.inp('vecs', [128, NV], F32)
        S.dma('sp', self.vecs[:], v, W=[self.vecs])
        g = DEPTH * LV
        tmp = S.tile([128, 16], F32, 'sctmp')
        S.act(tmp[:], self.vecs[:, g:g + 16], AF.Sigmoid, R=[self.vecs], W=[tmp])
        S.tt('dve', self.sc[:, :, 0], tmp[:, 0:8], self.vecs[:, g:g + 8], ALU.mult, R=[tmp, self.vecs], W=[self.sc])
        S.tt('dve', self.sc[:, :, 1], tmp[:, 8:16], self.vecs[:, g + 8:g + 16], ALU.mult, R=[tmp, self.vecs], W=[self.sc])
        self.res_mark = S.mark()

    def tiles(self, T):
        r = [(o, min(T, self.Lc - o), 0) for o in range(0, self.Lc, T)]
        for o in range(0, NCTX, T):
            r.append((self.Lc + o, min(T, NCTX - o), 1))
        return r

    def wload(self, name, rows, cols, c0=0, c1=None, dtype=BF16, tname=None):
        S = self.S
        c1 = cols if c1 is None else c1
        w = self.env.inp(name, [rows, cols], F32)
        kc = rows // 128
        t = S.tile([128, kc, c1 - c0], dtype, tname or name)
        src = w.rearrange('(k p) c -> p k c', p=128)[:, :, c0:c1]
        S.dma('pool' if dtype != F32 else 'sp', t[:], src, W=[t])
        return t

    def adaln(self, i):
        S = self.S
        m = S.mark()
        aw = self.env.inp('ada_w%d' % i, [D, 6 * D], F32)
        src = aw.rearrange('(k p) c -> p k c', p=128)
        wts = [S.tile([128, 8, 1024], F32, 'adaw%d' % q) for q in range(2)]
        for piece in range(6):
            wt = wts[piece % 2]
            S.dma('sp', wt[:], src[:, :, piece * 1024:(piece + 1) * 1024], W=[wt])
            ps = S.psum()
            for f in range(8):
                for k in range(8):
                    S.mm(ps[:, f * 2:f * 2 + 2], wt[:, k, f * 128:(f + 1) * 128], self.sc[:, k, :],
                         k == 0, k == 7, R=[wt, self.sc], W=[ps])
            for j in range(2):
                S.tt('dve', self.mod[:, piece * 8:(piece + 1) * 8, j],
                     ps[:, 0:16].rearrange('p (f j) -> p f j', j=2)[:, :, j],
                     self.vecs[:, i * LV + piece * 8:i * LV + piece * 8 + 8], ALU.add,
                     R=[ps, self.vecs], W=[self.mod])
        for j in range(2):
            for n, (gcol, scale_p, shift_p) in enumerate(((48, 1, 0), (56, 4, 3))):
                S.stt('dve', self.ab[:, 2 * n, :, j], self.mod[:, scale_p * 8:scale_p * 8 + 8, j], 1.0,
                      self.vecs[:, i * LV + gcol:i * LV + gcol + 8], ALU.add, ALU.mult,
                      R=[self.mod, self.vecs], W=[self.ab])
                S.copy('dve', self.ab[:, 2 * n + 1, :, j], self.mod[:, shift_p * 8:shift_p * 8 + 8, j],
                       R=[self.mod], W=[self.ab])
        S.barrier()
        S.release(m)

    def rstd_of(self, chunks, Rtoks, T, dim, sq_tile, rstd_tile, sq_eng='pool'):
        S = self.S
        n = len(chunks)
        for k, c in enumerate(chunks):
            if sq_eng == 'act':
                S.act(sq_tile[:, k, :T], c, AF.Square, R=Rtoks, W=[sq_tile])
            else:
                S.tt(sq_eng, sq_tile[:, k, :T], c, c, ALU.mult, R=Rtoks, W=[sq_tile])
        ps = S.psum()
        for k in range(n):
            S.mm(ps[:, :T], self.ones[:], sq_tile[:, k, :T], k == 0, k == n - 1, R=[self.ones, sq_tile], W=[ps])
        S.act(rstd_tile[:, :T], ps[:, :T], AF.Sqrt, R=[ps], W=[rstd_tile], scale=1.0 / dim, bias=EPS)
        S.recip(rstd_tile[:, :T], rstd_tile[:, :T], R=[rstd_tile], W=[rstd_tile])

    def norm_mod(self, xt, T, which, j, hT, sq, rstd, tmp):
        S = self.S
        self.rstd_of([xt[:, k, :T] for k in range(8)], [xt], T, D, sq, rstd)
        for k in range(8):
            tp = tmp[k % 2]
            S.tt('dve', tp[:, :T], xt[:, k, :T], rstd[:, :T], ALU.mult, R=[xt, rstd], W=[tp])
            S.act(hT[:, k, :T], tp[:, :T], AF.Identity, R=[tp, self.ab], W=[hT],
                  scale=self.ab[:, 2 * which, k, j:j + 1], bias=self.ab[:, 2 * which + 1, k, j:j + 1])

    def work_tiles(self, T, tmp=True):
        S = self.S
        w = {}
        w['xt'] = S.tile([128, 8, T], F32, 'xt')
        w['hT'] = S.tile([128, 8, T], BF16, 'hT')
        w['sq'] = S.tile([128, 8, T], BF16, 'sq')
        w['rstd'] = S.tile([128, T], F32, 'rstd')
        if tmp:
            w['tmp'] = [S.tile([128, T], F32, 'tmp%d' % q) for q in range(2)]
        w['xt2'] = S.tile([128, 8, T], F32, 'xt2')
        return w

    def xiter(self, tl, xs, w):
        S = self.S
        xts = [w['xt'], w['xt2']]

        def load(n):
            off, tw, j = tl[n]
            S.dma('sp', xts[n % 2][:, :, :tw], xs[:, :, off:off + tw], W=[xts[n % 2]])
        if tl:
            load(0)
        for n, t in enumerate(tl):
            if n + 1 < len(tl):
                load(n + 1)
            yield xts[n % 2], t

    def xs_view(self, ap):
        return ap.rearrange('(k p) t -> p k t', p=128)

    def mlp(self, w, T, j, w1, w2, h1, rl):
        S = self.S
        hT, xt = w['hT'], w['xt']
        for half in range(2):
            for fl in range(16):
                f = half * 16 + fl
                ps = S.psum()
                for k in range(8):
                    S.mm(ps[:, :T], w1[:, k, f * 128:(f + 1) * 128], hT[:, k, :T], k == 0, k == 7, R=[w1, hT], W=[ps])
                r = rl[f % 2]
                S.act(r[:, :T], ps[:, :T], AF.Relu, R=[ps], W=[r])
                S.tt('pool' if f % 2 else 'dve', h1[:, fl, :T], r[:, :T], r[:, :T], ALU.mult, R=[r], W=[h1.toks[fl]])
            for o in range(8):
                ps = S.psum()
                for fl in range(16):
                    f = half * 16 + fl
                    S.mm(ps[:, :T], w2[:, f, o * 128:(o + 1) * 128], h1[:, fl, :T], fl == 0, fl == 15, R=[w2, h1.toks[fl]], W=[ps])
                S.stt('dve', xt[:, o, :T], ps[:, :T], self.mod[:, 40 + o, j:j + 1], xt[:, o, :T], ALU.mult, ALU.add,
                      R=[ps, self.mod, xt], W=[xt])

    def phase_M(self, i, final):
        S = self.S
        env = self.env
        NT = self.NT
        T = 512
        m = S.mark()
        xs = self.xs_view(env.w('xs', [D, NT], F32))
        w1 = self.wload('mlp_w1_%d' % i, D, DFF)
        w2 = self.wload('mlp_w2_%d' % i, DFF, D)
        w = self.work_tiles(T, tmp=False)
        h1 = S.tile([128, 16, T], BF16, 'h1')
        h1.toks = [Tok() for _ in range(16)]
        rl = [S.tile([128, T], F32, 'rl%d' % q) for q in range(2)]
        w['tmp'] = rl
        if final:
            fin = env.w('yT', [D, self.Lc], F32, external=True).rearrange('(k p) t -> p k t', p=128)
            gF = DEPTH * LV + 16
        for xt, (off, tw, j) in self.xiter([t for t in self.tiles(T) if not (final and t[2] == 1)], xs, w):
            w['xt'] = xt
            self.norm_mod(xt, tw, 1, j, w['hT'], w['sq'], w['rstd'], w['tmp'])
            self.mlp(w, tw, j, w1, w2, h1, rl)
            if final:
                self.rstd_of([xt[:, k, :tw] for k in range(8)], [xt], tw, D, w['sq'], w['rstd'])
                for k in range(8):
                    S.stt('dve', xt[:, k, :tw], xt[:, k, :tw], self.vecs[:, gF + k:gF + k + 1], w['rstd'][:, :tw],
                          ALU.mult, ALU.mult, R=[xt, self.vecs, w['rstd']], W=[xt])
                S.dma('sp', fin[:, :, off:off + tw], xt[:, :, :tw], R=[xt])
            else:
                S.dma('sp', xs[:, :, off:off + tw], xt[:, :, :tw], R=[xt])
        S.barrier()
        S.release(m)

    def phase_E3(self, i):
        S = self.S
        env = self.env
        NT = self.NT
        T = 512
        m = S.mark()
        xs = self.xs_view(env.w('xs', [D, NT], F32))
        wo = self.wload('ev_w_out%d' % (i // 2), D, D)
        ats = env.r('ATs', [512, NT], BF16).rearrange('(k p) t -> p k t', p=128)
        mls = env.r('MLs', [512, NT], BF16).rearrange('(k p) t -> p k t', p=128)
        xts = [S.tile([128, 8, T], F32, 'xt%d' % q) for q in range(2)]
        mixs = [S.tile([128, 8, T], BF16, 'mix%d' % q) for q in range(2)]
        for n, (off, tw, j) in enumerate(self.tiles(T)):
            xt, mix = xts[n % 2], mixs[n % 2]
            S.dma('sp', xt[:, :, :tw], xs[:, :, off:off + tw], W=[xt])
            S.dma('sp', mix[:, 0:4, :tw], ats[:, :, off:off + tw], W=[mix])
            S.dma('sp', mix[:, 4:8, :tw], mls[:, :, off:off + tw], W=[mix])
            for o in range(8):
                ps = S.psum()
                for k in range(8):
                    S.mm(ps[:, :tw], wo[:, k, o * 128:(o + 1) * 128], mix[:, k, :tw], k == 0, k == 7, R=[wo, mix], W=[ps])
                S.stt('dve', xt[:, o, :tw], ps[:, :tw], self.mod[:, 16 + o, j:j + 1], xt[:, o, :tw], ALU.mult, ALU.add,
                      R=[ps, self.mod, xt], W=[xt])
            S.dma('sp', xs[:, :, off:off + tw], xt[:, :, :tw], R=[xt])
        S.barrier()
        S.release(m)

    def phase_E1(self, i):
        S = self.S
        env = self.env
        NT = self.NT
        T = 512
        jx = i // 2
        m = S.mark()
        vb = i * LV + 64
        xs = self.xs_view(env.r('xs', [D, NT], F32))
        Qs = env.w('Qs', [96, 8, NT], BF16)
        LATs = env.w('LATs', [256, NT], BF16).rearrange('(k p) t -> p k t', p=128)
        KRs = env.w('KRs', [32, NT], BF16)
        MLs = env.w('MLs', [512, NT], BF16).rearrange('(k p) t -> p k t', p=128)
        ropeQ = env.inp('ropeQ', [96, 2, NT], F32)
        ropeK = env.inp('ropeK', [32, 2, NT], F32)
        win = self.wload('ev_w_in%d' % jx, D, EVEN_IN)
        wkp = self.wload('ev_w_in_kp%d' % jx, D, 32)
        wuq = self.wload('ev_w_uq%d' % jx, 384, 768)
        wuqp = self.wload('ev_w_uq_p%d' % jx, 384, 768)
        wsT = S.tile([128, 4, 128], BF16, 'wsT')
        S.dma('pool', wsT[:], env.inp('ev_wsT%d' % jx, [128, 4, 128], F32), W=[wsT])
        bc = S.tile([128, 640], F32, 'bcE')
        S.dma('sp', bc[:], env.inp('ev_bc%d' % jx, [128, 640], F32), W=[bc])
        w = self.work_tiles(T)
        xt, hT, sq, rstd = w['xt'], w['hT'], w['sq'], w['rstd']
        tq = S.tile([96, 2, T], F32, 'tq')
        tk = S.tile([32, 2, T], F32, 'tk')
        cqn = S.tile([128, 3, T], BF16, 'cqn')
        qst = S.tile([96, 8, T], BF16, 'qst')
        qtmp = [S.tile([96, 2, T], F32, 'qtmp%d' % q) for q in range(2)]
        lat = S.tile([128, 2, T], BF16, 'lat')
        kr = S.tile([32, T], BF16, 'kr')
        u = S.tile([128, 4, T], F32, 'u')
        gv = S.tile([128, 512], F32, 'gv')
        junk = S.tile([128, 128], BF16, 'junk')
        ss = S.tile([128, 8], F32, 'ss')
        vn = S.tile([128, 4, 128], BF16, 'vn')
        t2 = S.tile([128, 512], F32, 't2')
        ml = S.tile([128, 4, T], BF16, 'ml')
        for xt, (off, tw, j) in self.xiter(self.tiles(T), xs, w):
            S.dma('sp', tq[:, :, :tw], ropeQ[:, :, off:off + tw], W=[tq])
            S.dma('sp', tk[:, :, :tw], ropeK[:, :, off:off + tw], W=[tk])
            self.norm_mod(xt, tw, 0, j, hT, sq, rstd, w['tmp'])
            pcs = []
            for c in range(3):
                ps = S.psum()
                for k in range(8):
                    S.mm(ps[:, :tw], win[:, k, c * 128:(c + 1) * 128], hT[:, k, :tw], k == 0, k == 7, R=[win, hT], W=[ps])
                pcs.append(ps)
            self.rstd_of([p[:, :tw] for p in pcs], pcs, tw, 384, sq, rstd, sq_eng='act')
            for c in range(3):
                S.stt('dve', cqn[:, c, :tw], pcs[c][:, :tw], self.vecs[:, vb + c:vb + c + 1], rstd[:, :tw], ALU.mult, ALU.mult,
                      R=[pcs[c], self.vecs, rstd], W=[cqn])
            pcs = []
            for c in range(2):
                ps = S.psum()
                for k in range(8):
                    S.mm(ps[:, :tw], win[:, k, E_Q + c * 128:E_Q + (c + 1) * 128], hT[:, k, :tw], k == 0, k == 7, R=[win, hT], W=[ps])
                pcs.append(ps)
            self.rstd_of([p[:, :tw] for p in pcs], pcs, tw, 256, sq, rstd, sq_eng='act')
            for c in range(2):
                S.stt('dve', lat[:, c, :tw], pcs[c][:, :tw], self.vecs[:, vb + 3 + c:vb + 4 + c], rstd[:, :tw], ALU.mult, ALU.mult,
                      R=[pcs[c], self.vecs, rstd], W=[lat])
            S.dma('sp', LATs[:, :, off:off + tw], lat[:, :, :tw], R=[lat])
            pa, pb = S.psum(), S.psum()
            for k in range(8):
                S.mm(pa[0:32, :tw], win[:, k, E_KV:E_R], hT[:, k, :tw], k == 0, k == 7, R=[win, hT], W=[pa])
            for k in range(8):
                S.mm(pb[0:32, :tw], wkp[:, k, :], hT[:, k, :tw], k == 0, k == 7, R=[wkp, hT], W=[pb])
            qt = qtmp[0]
            S.tt('dve', qt[0:32, 0, :tw], pa[0:32, :tw], tk[:, 0, :tw], ALU.mult, R=[pa, tk], W=[qt])
            S.tt('dve', qt[0:32, 1, :tw], pb[0:32, :tw], tk[:, 1, :tw], ALU.mult, R=[pb, tk], W=[qt])
            S.tt('pool', kr[:, :tw], qt[0:32, 0, :tw], qt[0:32, 1, :tw], ALU.add, R=[qt], W=[kr])
            S.dma('sp', KRs[:, off:off + tw], kr[:, :tw], R=[kr])
            for g in range(4):
                ps = S.psum()
                for k in range(8):
                    S.mm(ps[:, :tw], win[:, k, E_R + g * 128:E_R + (g + 1) * 128], hT[:, k, :tw], k == 0, k == 7, R=[win, hT], W=[ps])
                S.act(u[:, g, :tw], ps[:, :tw], AF.Gelu_apprx_tanh, R=[ps], W=[u])
            for cc in range(tw // 128):
                ps = S.psum()
                for k in range(8):
                    S.mm(ps[:, :], hT[:, k, cc * 128:(cc + 1) * 128], win[:, k, E_U:EVEN_IN], k == 0, k == 7, R=[win, hT], W=[ps])
                S.act(gv[:], ps[:], AF.Gelu_apprx_tanh, R=[ps], W=[gv])
                for g in range(4):
                    S.act(junk[:], gv[:, g * 128:(g + 1) * 128], AF.Square, R=[gv], W=[junk, ss], accum_out=ss[:, g:g + 1])
                S.act(ss[:, 4:8], ss[:, 0:4], AF.Sqrt, R=[ss], W=[ss], scale=1.0 / 128, bias=EPS)
                S.recip(ss[:, 4:8], ss[:, 4:8], R=[ss], W=[ss])
                for g in range(4):
                    S.stt('dve', vn[:, g, :], gv[:, g * 128:(g + 1) * 128], ss[:, 4 + g:5 + g], bc[:, 0:128], ALU.mult, ALU.mult,
                          R=[gv, ss, bc], W=[vn])
                po = S.psum()
                for g in range(4):
                    S.mm(po[:, g * 128:(g + 1) * 128], vn[:, g, :], wsT[:, g, :], True, True, R=[vn, wsT], W=[po])
                S.tt('dve', t2[:], po[:], bc[:, 128:640], ALU.add, R=[po, bc], W=[t2])
                S.tt('pool', ml[:, :, cc * 128:(cc + 1) * 128], t2[:].rearrange('p (g t) -> p g t', g=4),
                     u[:, :, cc * 128:(cc + 1) * 128], ALU.mult, R=[t2, u], W=[ml])
            for h in range(8):
                pa, pb = S.psum(), S.psum()
                for c in range(3):
                    S.mm(pa[0:96, :tw], wuq[:, c, h * 96:(h + 1) * 96], cqn[:, c, :tw], c == 0, c == 2, R=[wuq, cqn], W=[pa])
                for c in range(3):
                    S.mm(pb[0:96, :tw], wuqp[:, c, h * 96:(h + 1) * 96], cqn[:, c, :tw], c == 0, c == 2, R=[wuqp, cqn], W=[pb])
                qt = qtmp[h % 2]
                S.tt('dve', qt[:, 0, :tw], pa[0:96, :tw], tq[:, 0, :tw], ALU.mult, R=[pa, tq], W=[qt])
                S.tt('dve', qt[:, 1, :tw], pb[0:96, :tw], tq[:, 1, :tw], ALU.mult, R=[pb, tq], W=[qt])
                S.tt('pool', qst[:, h, :tw], qt[:, 0, :tw], qt[:, 1, :tw], ALU.add, R=[qt], W=[qst])
            S.dma('sp', Qs[:, :, off:off + tw], qst[:, :, :tw], R=[qst])
            S.dma('sp', MLs[:, :, off:off + tw], ml[:, :, :tw], R=[ml])
        S.barrier()
        S.release(m)

    def phase_E2(self, i):
        S = self.S
        env = self.env
        NT = self.NT
        NKEY = self.G * self.Lc + NCTX
        NKB = NKEY // 128
        T = 512
        jx = i // 2
        m = S.mark()
        Qs = env.r('Qs', [96, 8, NT], BF16)
        ATs = env.w('ATs', [512, NT], BF16)
        LATall = env.r('LATall' if self.G > 1 else 'LATs', [256, NKEY], BF16).rearrange('(k p) t -> p k t', p=128)
        KRall = env.r('KRall' if self.G > 1 else 'KRs', [32, NKEY], BF16)
        wukv = self.wload('ev_w_ukv%d' % jx, 256, 1024)
        lat = S.tile([128, 2, NKEY], BF16, 'latall')
        KT = S.tile([96, NKEY], BF16, 'KT')
        KTr = Tok('KTr')
        VA = S.tile([128, NKB, 128], BF16, 'VA')
        qTs = [S.tile([96, T], BF16, 'qT%d' % q) for q in range(2)]
        PTs = [S.tile([128, 2, T], BF16, 'PT%d' % q) for q in range(3)]
        rec = S.tile([64, T], F32, 'rec')
        ast = [S.tile([64, T], BF16, 'ast%d' % q) for q in range(2)]
        for c in range(2):
            S.dma('sp', lat[:, c, :], LATall[:, c, :], W=[lat])
        S.dma('sp', KT[64:96, :], KRall, W=[KTr], semtok=KTr)
        S.memset('pool', VA[:, :, 64:128], 1.0, W=[VA])
        qn = 0
        for h in range(8):
            for n, k0 in enumerate(range(0, NKEY, 512)):
                kn = min(512, NKEY - k0)
                ps = S.psum(n % 6)
                for c in range(2):
                    S.mm(ps[0:64, :kn], wukv[:, c, h * 128:h * 128 + 64], lat[:, c, k0:k0 + kn], c == 0, c == 1, R=[wukv, lat], W=[ps])
                S.copy('dve', KT[0:64, k0:k0 + kn], ps[0:64, :kn], R=[ps], W=[KT])
            for n, b0 in enumerate(range(0, NKB, 8)):
                nb = min(8, NKB - b0)
                ps = S.psum(n % 6)
                for b in range(nb):
                    for c in range(2):
                        S.mm(ps[:, b * 64:(b + 1) * 64], lat[:, c, (b0 + b) * 128:(b0 + b + 1) * 128],
                             wukv[:, c, h * 128 + 64:h * 128 + 128], c == 0, c == 1, R=[wukv, lat], W=[ps])
                S.copy('dve', VA[:, b0:b0 + nb, 0:64],
                       ps[:, 0:nb * 64].rearrange('p (b d) -> p b d', d=64), R=[ps], W=[VA])
            for (off, tw, j) in self.tiles(T):
                blocks = list(range(NKB)) if j == 0 else list(range(NKB - NCTX // 128, NKB))
                qT = qTs[qn % 2]
                psO = S.psum(6 + qn % 2)
                a_st = ast[qn % 2]
                qn += 1
                S.dma('sp', qT[:, :tw], Qs[:, h, off:off + tw], W=[qT])
                npair = len(blocks) // 2
                LA = 2

                def qk(pi):
                    for u in range(2):
                        kb = blocks[2 * pi + u]
                        bk = S.banks[(pi % 3) * 2 + u]
                        S.mm(bk[:, :tw], KT[0:96, kb * 128:(kb + 1) * 128], qT[:, :tw], True, True, R=[KT, KTr, qT], W=[bk])
                for pi in range(min(LA, npair)):
                    qk(pi)
                for pi in range(npair):
                    PT = PTs[pi % 3]
                    pr = S.pairs[pi % 3]
                    b0, b1 = S.banks[(pi % 3) * 2], S.banks[(pi % 3) * 2 + 1]
                    S.act(PT[:, :, :tw], pr[:, :, :tw], AF.Exp, R=[b0, b1], W=[PT])
                    if pi + LA < npair:
                        qk(pi + LA)
                    for u in range(2):
                        S.mm(psO[:, :tw], VA[:, blocks[2 * pi + u], :], PT[:, u, :tw], pi == 0 and u == 0,
                             pi == npair - 1 and u == 1, R=[VA, PT], W=[psO])
                S.recip(rec[:, :tw], psO[64:128, :tw], R=[psO], W=[rec])
                S.tt('dve', a_st[:, :tw], psO[0:64, :tw], rec[:, :tw], ALU.mult, R=[psO, rec], W=[a_st])
                S.dma('sp', ATs[h * 64:(h + 1) * 64, off:off + tw], a_st[:, :tw], R=[a_st])
        S.barrier()
        S.release(m)

    def gla_consts(self):
        S = self.S
        cst = S.tile([128, 6, 128], F32, 'glacst')
        S.dma('sp', cst[:], self.env.r('gla_cst', [128, 6, 128], F32), W=[cst])
        return cst

    def gla_gate_tiles(self, jx, T):
        S = self.S
        g = {}
        for d in 'fb':
            t = S.tile([17, 512], F32, 'wg' + d)
            S.dma('sp', t[:], self.env.r('od_wg%s%d' % (d, jx), [17, 512], F32), W=[t])
            g['wg' + d] = t
            z = S.tile([17, T], F32, 'za' + d)
            S.memset('pool', z[:], 1.0, W=[z])
            g['za' + d] = z
        g['etmp'] = S.tile([128, 512], F32, 'etmp')
        g['Ee'] = S.tile([128, 512], F32, 'Ee')
        g['kend'] = S.tile([128, 512], BF16, 'kend')
        g['c'] = []
        for p in range(2):
            c = {}
            c['lf'] = S.tile([128, 512], F32, 'lf%d' % p)
            c['lb'] = S.tile([128, 512], F32, 'lb%d' % p)
            c['vbf'] = S.tile([128, 1024], BF16, 'vbf%d' % p)
            c['dec'] = S.tile([128, 8], F32, 'dec%d' % p)
            g['c'].append(c)
        return g

    def gla_z(self, g, win, hT, tw):
        S = self.S
        for n, d in enumerate('fb'):
            ps = S.psum()
            c0 = O_V + 16 * n
            for k in range(8):
                S.mm(ps[0:16, :tw], win[:, k, c0:c0 + 16], hT[:, k, :tw], k == 0, k == 7, R=[win, hT], W=[ps])
            S.copy('dve', g['za' + d][0:16, :tw], ps[0:16, :tw], R=[ps], W=[g['za' + d]])

    def gla_tokmajor(self, g, gc, win, hT, c0, bank):
        S = self.S
        psk = S.psum(bank)
        for k in range(8):
            S.mm(psk[:], hT[:, k, c0:c0 + 128], win[:, k, 0:O_K], k == 0, k == 7, R=[win, hT], W=[psk])
        for q in range(2):
            ps = S.psum()
            for k in range(8):
                S.mm(ps[:], hT[:, k, c0:c0 + 128], win[:, k, O_K + q * 512:O_K + (q + 1) * 512], k == 0, k == 7, R=[win, hT], W=[ps])
            S.copy('act' if q else 'dve', gc['vbf'][:, q * 512:(q + 1) * 512], ps[:], R=[ps], W=[gc['vbf']])
        for d in 'fb':
            ps = S.psum()
            S.mm(ps[:], g['za' + d][0:17, c0:c0 + 128], g['wg' + d][0:17, :], True, True, R=[g['za' + d], g['wg' + d]], W=[ps])
            S.act(g['etmp'][:], ps[:], AF.Exp, R=[ps], W=[g['etmp']], scale=-1.0)
            S.act(gc['l' + d][:], g['etmp'][:], AF.Ln, R=[g['etmp']], W=[gc['l' + d]], bias=1.0)
        ps = S.psum()
        for n, d in enumerate('fb'):
            for h in range(4):
                S.mm(ps[:, n * 4 + h:n * 4 + h + 1], gc['l' + d][:, h * 128:(h + 1) * 128], self.cst[:, 0, 127:128], True, True,
                     R=[gc['l' + d], self.cst], W=[ps])
        S.act(gc['dec'][:], ps[:, 0:8], AF.Exp, R=[ps], W=[gc['dec']])
        return psk

    def gla_dS(self, g, gc, psk, d):
        S = self.S
        ps = S.psum()
        S.mm(ps[:], self.cst[:, 2 if d == 'f' else 3, :], gc['l' + d][:], True, True, R=[self.cst, gc['l' + d]], W=[ps])
        S.act(g['Ee'][:], ps[:], AF.Exp, R=[ps], W=[g['Ee']])
        S.tt('dve', g['kend'][:], psk[:], g['Ee'][:], ALU.mult, R=[psk, g['Ee']], W=[g['kend']])
        banks = []
        for q in range(2):
            pd = S.psum()
            for hh in range(2):
                h = q * 2 + hh
                S.mm(pd[:, hh * 256:(hh + 1) * 256], g['kend'][:, h * 128:(h + 1) * 128], gc['vbf'][:, h * 256:(h + 1) * 256],
                     True, True, R=[g['kend'], gc['vbf']], W=[pd])
            banks.append(pd)
        return banks

    def phase_O1(self, i):
        S = self.S
        env = self.env
        NT, Lc = self.NT, self.Lc
        T = 256
        jx = i // 2
        NCH = NT // 128
        m = S.mark()
        S.rr_n = 6
        xs = self.xs_view(env.r('xs', [D, NT], F32))
        SBs = env.w('SBs', [128, NCH, 4, 256], BF16)
        DCs = env.w('DCs', [128, NCH, 4], F32)
        EXs = env.w('EXs', [128, 2, 4, 257], F32)
        CXs = env.w('CXs', [128, 2, 4, 256], F32)
        win = self.wload('od_w_in%d' % jx, D, ODD_IN, 0, O_ZB)
        self.cst = self.gla_consts()
        g = self.gla_gate_tiles(jx, T)
        w = self.work_tiles(T)
        xt, hT = w['xt'], w['hT']
        dcum = S.tile([128, NCH, 4], F32, 'dcum')
        sbst = [S.tile([128, 4, 256], BF16, 'sbst%d' % q) for q in range(2)]
        exs = S.tile([128, 2, 4, 257], F32, 'exs')
        nst = 0
        for chain in (1, 0):
            S.memset('pool', exs[:], 0.0, W=[exs])
            S.memset('pool', exs[:, :, :, 256:257], 1.0, W=[exs])
            tl = [t for t in self.tiles(T) if t[2] == chain]
            for xt, (off, tw, j) in self.xiter(list(reversed(tl)), xs, w):
                self.norm_mod(xt, tw, 0, j, hT, w['sq'], w['rstd'], w['tmp'])
                self.gla_z(g, win, hT, tw)
                chunks = list(reversed(range(tw // 128)))
                psks = [self.gla_tokmajor(g, g['c'][ci], win, hT, cc * 128, 7 - ci) for ci, cc in enumerate(chunks)]
                for ci, cc in enumerate(chunks):
                    gc, psk = g['c'][ci], psks[ci]
                    n = (off + cc * 128) // 128
                    st = sbst[nst % 2]
                    nst += 1
                    S.copy('pool', st[:], exs[:, 1, :, 0:256], R=[exs], W=[st])
                    S.dma('sp', SBs[:, n], st[:], R=[st])
                    S.copy('pool', dcum[:, n, :], exs[:, 1, :, 256], R=[exs], W=[dcum])
                    bk = self.gla_dS(g, gc, psk, 'b')
                    for h in range(4):
                        S.stt('dve', exs[:, 1, h, 0:256], exs[:, 1, h, 0:256], gc['dec'][:, 4 + h:5 + h],
                              bk[h // 2][:, (h % 2) * 256:(h % 2 + 1) * 256], ALU.mult, ALU.add, R=[exs, gc['dec'], bk[h // 2]], W=[exs])
                    S.tt('dve', exs[:, 1, :, 256], exs[:, 1, :, 256], gc['dec'][:, 4:8], ALU.mult, R=[exs, gc['dec']], W=[exs])
                    bk = self.gla_dS(g, gc, psk, 'f')
                    for h in range(4):
                        S.stt('dve', exs[:, 0, h, 0:256], bk[h // 2][:, (h % 2) * 256:(h % 2 + 1) * 256], exs[:, 0, h, 256:257],
                              exs[:, 0, h, 0:256], ALU.mult, ALU.add, R=[exs, bk[h // 2]], W=[exs])
                    S.tt('dve', exs[:, 0, :, 256], exs[:, 0, :, 256], gc['dec'][:, 0:4], ALU.mult, R=[exs, gc['dec']], W=[exs])
            if chain == 1:
                S.dma('sp', CXs, exs[:, :, :, 0:256], R=[exs])
            else:
                S.dma('sp', EXs, exs[:], R=[exs])
        S.dma('sp', DCs, dcum[:], R=[dcum])
        S.barrier()
        S.rr_n = 8
        S.release(m)

    def phase_O2(self, i):
        S = self.S
        env = self.env
        NT, Lc = self.NT, self.Lc
        T = 256
        jx = i // 2
        NCH = NT // 128
        LNQ = float(np.log(128.0 ** -0.5))
        m = S.mark()
        S.rr_n = 6
        vb = i * LV + 64
        xs = self.xs_view(env.w('xs', [D, NT], F32))
        SBs = env.r('SBs', [128, NCH, 4, 256], BF16)
        DCs = env.r('DCs', [128, NCH, 4], F32)
        CXs = env.r('CXs', [128, 2, 4, 256], F32)
        if self.G > 1:
            EXall = env.r('EXall', [128, 4, 2, 4, 257], F32)
        pmk = env.r('posmask', [128, 16], F32)
        win = self.wload('od_w_in%d' % jx, D, ODD_IN)
        wo = self.wload('od_w_out%d' % jx, D, D)
        self.cst = cst = self.gla_consts()
        g = self.gla_gate_tiles(jx, T)
        w = self.work_tiles(T)
        xt, hT = w['xt'], w['hT']
        mk4 = S.tile([128, 2, 4, 128], F32, 'mk4')
        for q in range(2):
            for h in range(4):
                S.copy('pool', mk4[:, q, h, :], cst[:, 4 + q, :], R=[cst], W=[mk4])
        dcum = S.tile([128, NCH, 4], F32, 'dcum')
        S.dma('sp', dcum[:], DCs, W=[dcum])
        pm = S.tile([128, 16], F32, 'pm')
        S.dma('sp', pm[:], pmk, W=[pm])
        Sin = S.tile([128, 2, 4, 256], F32, 'Sin')
        S.dma('sp', Sin[:], CXs, W=[Sin])
        ex = S.tile([128, 2, 4, 257], F32, 'ex')
        dp = S.tile([128, 4], F32, 'dp')
        sft = S.tile([128, 256], F32, 'sft')
        for d, order in (((0, range(4)), (1, reversed(range(4)))) if self.G > 1 else ()):
            for q in order:
                S.dma('sp', ex[:], EXall[:, q], W=[ex])
                mcol = pm[:, d * 4 + q:d * 4 + q + 1]
                S.ts('dve', dp[:], ex[:, d, :, 256], mcol, None, ALU.mult, None, R=[ex, pm], W=[dp])
                S.ts('dve', dp[:], dp[:], pm[:, 8 + d * 4 + q:9 + d * 4 + q], None, ALU.add, None, R=[dp, pm], W=[dp])
                for h in range(4):
                    S.ts('dve', sft[:], ex[:, d, h, 0:256], mcol, None, ALU.mult, None, R=[ex, pm], W=[sft])
                    S.stt('dve', Sin[:, d, h, :], Sin[:, d, h, :], dp[:, h:h + 1], sft[:], ALU.mult, ALU.add, R=[Sin, dp, sft], W=[Sin])
        Sf = S.tile([128, 4, 256], F32, 'Sf')
        Sfb = S.tile([128, 4, 256], BF16, 'Sfb')
        Sbl = S.tile([128, 4, 256], BF16, 'Sbl')
        Sbe = S.tile([128, 4, 256], BF16, 'Sbe')
        kTs = S.tile([128, 4, T], F32, 'kTs')
        qTs = S.tile([128, 4, T], F32, 'qTs')
        rs = S.tile([128, 2, 4, T], BF16, 'rs')
        mixT = S.tile([128, 2, 4, T], BF16, 'mixT')
        E4 = [S.tile([128, 4, 128], F32, 'E4_%d' % q) for q in range(4)]
        qk4s = [[S.tile([128, 4, 128], BF16, 'qk4_%d_%d' % (p, q)) for q in range(4)] for p in range(2)]
        A1 = S.tile([128, 4, 128], F32, 'A1')
        A2 = S.tile([128, 4, 128], F32, 'A2')
        Ab = S.tile([128, 4, 128], BF16, 'Ab')
        osq = [S.tile([128, 512], BF16, 'osq%d' % q) for q in range(2)]
        rso = S.tile([128, 4, 128], F32, 'rso')
        mtmp = S.tile([128, 4, 128], F32, 'mtmp')
        for chain in (1, 0):
            if chain == 1:
                S.memset('pool', Sf[:], 0.0, W=[Sf])
            else:
                S.copy('pool', Sf[:], Sin[:, 0], R=[Sin], W=[Sf])
            S.copy('pool', Sfb[:], Sf[:], R=[Sf], W=[Sfb])
            for xt, (off, tw, j) in self.xiter([t for t in self.tiles(T) if t[2] == chain], xs, w):
                self.norm_mod(xt, tw, 0, j, hT, w['sq'], w['rstd'], w['tmp'])
                self.gla_z(g, win, hT, tw)
                for h in range(4):
                    for dst, c0 in ((kTs, 0), (qTs, O_ZB)):
                        ps = S.psum()
                        for k in range(8):
                            S.mm(ps[:, :tw], win[:, k, c0 + h * 128:c0 + (h + 1) * 128], hT[:, k, :tw], k == 0, k == 7, R=[win, hT], W=[ps])
                        S.copy('act', dst[:, h, :tw], ps[:, :tw], R=[ps], W=[dst])
                for h in range(4):
                    for jj in range(2):
                        ps = S.psum()
                        c0 = O_Q + (2 * h + jj) * 128
                        for k in range(8):
                            S.mm(ps[:, :tw], win[:, k, c0:c0 + 128], hT[:, k, :tw], k == 0, k == 7, R=[win, hT], W=[ps])
                        S.act(rs[:, jj, h, :tw], ps[:, :tw], AF.Silu, R=[ps], W=[rs])
                nck = tw // 128
                psks = []
                for cc in range(nck):
                    c0 = cc * 128
                    gc = g['c'][cc]
                    qk4 = qk4s[cc]
                    psks.append(self.gla_tokmajor(g, gc, win, hT, c0, 7 - cc))
                    for n2, d in enumerate('fb'):
                        ps = S.psum()
                        for h in range(4):
                            S.mm(ps[:, h * 128:(h + 1) * 128], gc['l' + d][:, h * 128:(h + 1) * 128], cst[:, n2, :], True, True,
                                 R=[gc['l' + d], cst], W=[ps])
                        psv = ps[:].rearrange('p (h t) -> p h t', h=4)
                        S.act(E4[2 * n2][:], psv, AF.Exp, R=[ps], W=[E4[2 * n2]], bias=LNQ)
                        S.act(E4[2 * n2 + 1][:], psv, AF.Exp, R=[ps], W=[E4[2 * n2 + 1]], scale=-1.0)
                        S.tt('dve', qk4[2 * n2][:], qTs[:, :, c0:c0 + 128], E4[2 * n2][:], ALU.mult, R=[qTs, E4[2 * n2]], W=[qk4[2 * n2]])
                        S.tt('pool', qk4[2 * n2 + 1][:], kTs[:, :, c0:c0 + 128], E4[2 * n2 + 1][:], ALU.mult, R=[kTs, E4[2 * n2 + 1]], W=[qk4[2 * n2 + 1]])
                for cc in range(nck):
                    n = (off + cc * 128) // 128
                    c0 = cc * 128
                    gc, psk, qk4 = g['c'][cc], psks[cc], qk4s[cc]
                    pA = []
                    for n2 in range(2):
                        ps = S.psum()
                        for h in range(4):
                            S.mm(ps[:, h * 128:(h + 1) * 128], qk4[2 * n2 + 1][:, h, :], qk4[2 * n2][:, h, :], True, True,
                                 R=[qk4[2 * n2], qk4[2 * n2 + 1]], W=[ps])
                        pA.append(ps)
                    S.tt('dve', A1[:], pA[0][:].rearrange('p (h t) -> p h t', h=4), mk4[:, 0], ALU.mult, R=[pA[0], mk4], W=[A1])
                    S.tt('dve', A2[:], pA[1][:].rearrange('p (h t) -> p h t', h=4), mk4[:, 1], ALU.mult, R=[pA[1], mk4], W=[A2])
                    S.tt('pool', Ab[:], A1[:], A2[:], ALU.add, R=[A1, A2], W=[Ab])
                    S.dma('sp', Sbl[:], SBs[:, n], W=[Sbl])
                    for h in range(4):
                        if chain == 0:
                            S.stt('dve', Sbe[:, h, :], Sin[:, 1, h, :], dcum[:, n, h:h + 1], Sbl[:, h, :], ALU.mult, ALU.add,
                                  R=[Sin, dcum, Sbl], W=[Sbe])
                        else:
                            S.copy('dve', Sbe[:, h, :], Sbl[:, h, :], R=[Sbl], W=[Sbe])
                    po = []
                    for jj in range(2):
                        ps = S.psum()
                        for h in range(4):
                            o_ = ps[:, h * 128:(h + 1) * 128]
                            S.mm(o_, gc['vbf'][:, h * 256 + jj * 128:h * 256 + (jj + 1) * 128], Ab[:, h, :], True, False, R=[gc['vbf'], Ab], W=[ps])
                            S.mm(o_, Sfb[:, h, jj * 128:(jj + 1) * 128], qk4[0][:, h, :], False, False, R=[Sfb, qk4[0]], W=[ps])
                            S.mm(o_, Sbe[:, h, jj * 128:(jj + 1) * 128], qk4[2][:, h, :], False, True, R=[Sbe, qk4[2]], W=[ps])
                        po.append(ps)
                        S.act(osq[jj][:], ps[:], AF.Square, R=[ps], W=[osq[jj]])
                    pss = S.psum()
                    S.mm(pss[:], self.ones[:], osq[0][:], True, False, R=[self.ones, osq[0]], W=[pss])
                    S.mm(pss[:], self.ones[:], osq[1][:], False, True, R=[self.ones, osq[1]], W=[pss])
                    S.act(rso[:], pss[:].rearrange('p (h t) -> p h t', h=4), AF.Sqrt, R=[pss], W=[rso], scale=1.0 / 256, bias=EPS)
                    S.recip(rso[:], rso[:], R=[rso], W=[rso])
                    for jj in range(2):
                        S.stt('dve', mtmp[:].rearrange('p h t -> p (h t)'), po[jj][:], self.vecs[:, vb + jj:vb + jj + 1],
                              rso[:].rearrange('p h t -> p (h t)'), ALU.mult, ALU.mult, R=[po[jj], self.vecs, rso], W=[mtmp])
                        S.tt('pool', mixT[:, jj, :, c0:c0 + 128], mtmp[:], rs[:, jj, :, c0:c0 + 128], ALU.mult, R=[mtmp, rs], W=[mixT])
                    bk = self.gla_dS(g, gc, psk, 'f')
                    for h in range(4):
                        S.stt('dve', Sf[:, h, :], Sf[:, h, :], gc['dec'][:, h:h + 1], bk[h // 2][:, (h % 2) * 256:(h % 2 + 1) * 256],
                              ALU.mult, ALU.add, R=[Sf, gc['dec'], bk[h // 2]], W=[Sf])
                    S.copy('pool', Sfb[:], Sf[:], R=[Sf], W=[Sfb])
                for o in range(8):
                    ps = S.psum()
                    for kc in range(8):
                        S.mm(ps[:, :tw], wo[:, kc, o * 128:(o + 1) * 128], mixT[:, kc % 2, kc // 2, :tw], kc == 0, kc == 7, R=[wo, mixT], W=[ps])
                    S.stt('dve', xt[:, o, :tw], ps[:, :tw], self.mod[:, 16 + o, j:j + 1], xt[:, o, :tw], ALU.mult, ALU.add,
                          R=[ps, self.mod, xt], W=[xt])
                S.dma('sp', xs[:, :, off:off + tw], xt[:, :, :tw], R=[xt])
        S.barrier()
        S.rr_n = 8
        S.release(m)

from concourse.bass_utils import run_bass_kernel_spmd

ROPE_PERM = np.array(list(range(8, 16)) + list(range(0, 8)) + list(range(24, 32)) + list(range(16, 24)))
ROPE_SIGN = np.array([-1.0] * 8 + [1.0] * 8 + [-1.0] * 8 + [1.0] * 8, np.float32)


def fm(v):
    return np.ascontiguousarray(np.asarray(v, np.float32).reshape(-1, 128).T)


def rope_tables(pos):
    r = (pos // 64).astype(np.float32)
    col = (pos % 64).astype(np.float32)
    inv = (np.float32(10000.0) ** (-np.arange(0, 16, 2, dtype=np.float32) / np.float32(16))).astype(np.float32)
    ar = (r[:, None] * inv).astype(np.float32)
    ac = (col[:, None] * inv).astype(np.float32)
    ang = np.concatenate([ar, ar, ac, ac], axis=-1)
    return np.cos(ang).astype(np.float32), np.sin(ang).astype(np.float32)


def prep_stores(inp, Lc, G=4):
    NT = Lc + NCTX
    x = np.asarray(inp['x'], np.float32)
    ctx = np.asarray(inp['ctx'], np.float32)
    shared = {}
    for i in range(DEPTH):
        shared['ada_w%d' % i] = np.ascontiguousarray(inp['ada_w'][i], np.float32)
        shared['mlp_w1_%d' % i] = np.ascontiguousarray(inp['mlp_w1'][i], np.float32)
        shared['mlp_w2_%d' % i] = np.ascontiguousarray(inp['mlp_w2'][i], np.float32)
    for j in range(2):
        w_in = np.asarray(inp['ev_w_in'][j], np.float32)
        shared['ev_w_in%d' % j] = np.ascontiguousarray(w_in)
        shared['ev_w_in_kp%d' % j] = np.ascontiguousarray(w_in[:, E_KV + ROPE_PERM])
        wuq = np.asarray(inp['ev_w_uq'][j], np.float32)
        shared['ev_w_uq%d' % j] = np.ascontiguousarray(wuq)
        wp = wuq.reshape(384, 8, 96).copy()
        wp[:, :, 64:] = wuq.reshape(384, 8, 96)[:, :, 64 + ROPE_PERM]
        shared['ev_w_uq_p%d' % j] = np.ascontiguousarray(wp.reshape(384, 768))
        shared['ev_w_ukv%d' % j] = np.ascontiguousarray(inp['ev_w_ukv'][j], np.float32)
        shared['ev_wsT%d' % j] = np.ascontiguousarray(np.transpose(np.asarray(inp['ev_cm_ws'][j], np.float32), (2, 0, 1)))
        bc = np.zeros((128, 640), np.float32)
        bc[:, 0:128] = np.asarray(inp['ev_cm_norm'][j], np.float32)[None, :]
        bc[:, 128:640] = np.asarray(inp['ev_cm_bs'][j], np.float32).reshape(1, 512)
        shared['ev_bc%d' % j] = bc
        shared['ev_w_out%d' % j] = np.ascontiguousarray(inp['ev_w_out'][j], np.float32)
        shared['od_w_in%d' % j] = np.ascontiguousarray(inp['od_w_in'][j], np.float32)
        shared['od_w_out%d' % j] = np.ascontiguousarray(inp['od_w_out'][j], np.float32)
        for d, wn, bn in (('f', 'od_w_gf', 'od_b_gf'), ('b', 'od_w_gb', 'od_b_gb')):
            a = np.zeros((17, 512), np.float32)
            a[0:16] = np.asarray(inp[wn][j], np.float32)
            a[16] = np.asarray(inp[bn][j], np.float32)
            shared['od_wg%s%d' % (d, j)] = a
        ob = np.zeros((128, 256), np.float32)
        ob[:, :] = np.asarray(inp['od_o_norm'][j], np.float32)[None, :]
        shared['od_bc%d' % j] = ob
    s_, t_ = np.meshgrid(np.arange(128), np.arange(128), indexing='ij')
    gc = np.zeros((128, 6, 128), np.float32)
    gc[:, 0] = np.where(s_ <= t_, -1.0 / 16, 0.0)
    gc[:, 1] = np.where(s_ >= t_, -1.0 / 16, 0.0)
    gc[:, 2] = np.where(s_ > t_, -1.0 / 16, 0.0)
    gc[:, 3] = np.where(s_ < t_, -1.0 / 16, 0.0)
    gc[:, 4] = np.where(s_ <= t_, 1.0, 0.0)
    gc[:, 5] = np.where(s_ >= t_, 1.0, 0.0)
    shared['gla_cst'] = gc
    stores = []
    for core in range(8):
        b, jc = core // 4, (core % 4 if G > 1 else 0)
        st = dict(shared)
        xs = np.concatenate([x[b, jc * Lc:(jc + 1) * Lc, :].T, ctx[b].T], axis=1)
        st['xs'] = np.ascontiguousarray(xs)
        vecs = np.zeros((128, NV), np.float32)
        for i in range(DEPTH):
            o = i * LV
            vecs[:, o:o + 48] = fm(inp['ada_b'][i])
            vecs[:, o + 48:o + 56] = fm(inp['norm1_g'][i])
            vecs[:, o + 56:o + 64] = fm(inp['norm2_g'][i])
            if i % 2 == 0:
                vecs[:, o + 64:o + 67] = fm(inp['ev_q_norm'][i // 2])
                vecs[:, o + 67:o + 69] = fm(inp['ev_kv_norm'][i // 2])
            else:
                vecs[:, o + 64:o + 66] = fm(inp['od_o_norm'][i // 2])
        g = DEPTH * LV
        vecs[:, g:g + 8] = fm(inp['c'][b])
        vecs[:, g + 8:g + 16] = fm(inp['c_ctx'])
        vecs[:, g + 16:g + 24] = fm(inp['final_g'])
        st['vecs'] = vecs
        cos, sin = rope_tables(np.arange(jc * Lc, (jc + 1) * Lc))
        cos = np.concatenate([cos, np.ones((NCTX, 32), np.float32)], 0).T
        sin = np.concatenate([sin * ROPE_SIGN[None, :], np.zeros((NCTX, 32), np.float32)], 0).T
        rq = np.zeros((96, 2, NT), np.float32)
        rq[0:64, 0, :] = MLA_SCALE
        rq[64:96, 0, :] = MLA_SCALE * cos
        rq[64:96, 1, :] = MLA_SCALE * sin
        st['ropeQ'] = rq
        rk = np.zeros((32, 2, NT), np.float32)
        rk[:, 0, :] = cos
        rk[:, 1, :] = sin
        st['ropeK'] = rk
        pm = np.zeros((128, 16), np.float32)
        for q in range(4):
            pm[:, q] = 1.0 if q < jc else 0.0
            pm[:, 4 + q] = 1.0 if q > jc else 0.0
        pm[:, 8:16] = 1.0 - pm[:, 0:8]
        st['posmask'] = pm
        stores.append(st)
    return stores


def build_program(phases, Lc, fused=False, G=4):
    nc = bass.Bass('TRN2', target_bir_lowering=False)
    b = B(nc, Lc, fused, G)
    S = b.S
    writes_xs = any(p[0] in ('E3', 'M', 'O2') for p in phases)
    if writes_xs:
        xin = b.env.r('xs_in', [D, b.NT], F32)
        xo = b.env.w('xs', [D, b.NT], F32)
        tk = Tok('xscopy')
        for k in range(8):
            S.dma('sp', xo[k * 128:(k + 1) * 128, :], xin[k * 128:(k + 1) * 128, :], W=[tk])
        S.barrier()
    for p in phases:
        getattr(b, 'adaln' if p[0] == 'adaln' else 'phase_' + p[0])(*p[1:])
    S.barrier()
    S.emit()
    return nc, b.env


def run_launch(phases, Lc, stores, trace=False):
    nc, env = build_program(phases, Lc, fused=False)
    in_maps = []
    for st in stores:
        im = {}
        for name in env.ins:
            src = 'xs' if name == 'xs_in' else name
            im[name] = st[src]
        in_maps.append(im)
    res = run_bass_kernel_spmd(nc, in_maps, core_ids=list(range(len(stores))), trace=trace)
    for st, r in zip(stores, res.results):
        for name in env.outs:
            st[name] = np.asarray(r[name])
    return res


def exchange_even(stores, Lc):
    for b in range(len(stores) // 4):
        grp = stores[b * 4:(b + 1) * 4]
        for name, allname in (('LATs', 'LATall'), ('KRs', 'KRall')):
            lat = np.concatenate([g[name][:, :Lc] for g in grp] + [grp[0][name][:, Lc:]], axis=1)
            for g in grp:
                g[allname] = lat


def exchange_odd(stores):
    for b in range(len(stores) // 4):
        grp = stores[b * 4:(b + 1) * 4]
        ex = np.ascontiguousarray(np.stack([g['EXs'] for g in grp], axis=1))
        for g in grp:
            g['EXall'] = ex


LAUNCHES = [
    ([('adaln', 0), ('E1', 0)], 'even'),
    ([('adaln', 0), ('E2', 0), ('E3', 0), ('M', 0, False), ('adaln', 1), ('O1', 1)], 'odd'),
    ([('adaln', 1), ('O2', 1), ('M', 1, False), ('adaln', 2), ('E1', 2)], 'even'),
    ([('adaln', 2), ('E2', 2), ('E3', 2), ('M', 2, False), ('adaln', 3), ('O1', 3)], 'odd'),
    ([('adaln', 3), ('O2', 3), ('M', 3, True)], None),
]


def kernel_unfused(**inputs):
    x = np.asarray(inputs['x'])
    Bn, L, _ = x.shape
    Lc = L // 4
    stores = prep_stores(inputs, Lc)
    for phases, ex in LAUNCHES:
        run_launch(phases, Lc, stores)
        if ex == 'even':
            exchange_even(stores, Lc)
        elif ex == 'odd':
            exchange_odd(stores)
    out = np.zeros((Bn, L, D), np.float32)
    for core in range(8):
        b, jc = core // 4, core % 4
        out[b, jc * Lc:(jc + 1) * Lc, :] = np.asarray(stores[core]['yT'], np.float32).T
    return out


FUSED_PHASES = [
    ('adaln', 0), ('E1', 0), ('E2', 0), ('E3', 0), ('M', 0, False),
    ('adaln', 1), ('O1', 1), ('O2', 1), ('M', 1, False),
    ('adaln', 2), ('E1', 2), ('E2', 2), ('E3', 2), ('M', 2, False),
    ('adaln', 3), ('O1', 3), ('O2', 3), ('M', 3, True),
]


def kernel(**inputs):
    x = np.asarray(inputs['x'])
    Bn, L, _ = x.shape
    stores = prep_stores(inputs, L, G=1)
    nc, env = build_program(FUSED_PHASES, L, fused=True, G=1)
    use = [b * 4 for b in range(Bn)]
    in_maps = [{name: stores[c]['xs' if name == 'xs_in' else name] for name in env.ins} for c in use]
    res = run_bass_kernel_spmd(nc, in_maps, core_ids=list(range(len(use))))
    out = np.zeros((Bn, L, D), np.float32)
    for b in range(Bn):
        out[b] = np.asarray(res.results[b]['yT'], np.float32).T
    return out
```

```python
import numpy as np
import concourse.bass as bass
import concourse.mybir as mybir

F32 = mybir.dt.float32
BF16 = mybir.dt.bfloat16
AF = mybir.ActivationFunctionType
ALU = mybir.AluOpType
AX = mybir.AxisListType

ENGS = ('pe', 'act', 'dve', 'pool', 'sp')
N_DMA_SEMS = 56


class Tok:
    __slots__ = ('w', 'r', 'dsem', 'name')

    def __init__(self, name=''):
        self.w = None
        self.r = []
        self.dsem = None
        self.name = name


class Tile:
    def __init__(self, t, tok=None, name=''):
        self.t = t
        self.tok = tok or Tok(name)

    def __getitem__(self, idx):
        return self.t[idx]


class BankView:
    def __init__(self, h, j):
        self.h, self.j = h, j

    def __getitem__(self, idx):
        if not isinstance(idx, tuple):
            idx = (idx, slice(None))
        return self.h[(idx[0], self.j) + tuple(idx[1:])]


class Sched:
    def __init__(self, nc):
        self.nc = nc
        self.ops = {e: [] for e in ENGS}
        self.sem = {e: nc.alloc_semaphore('s_' + e) for e in ENGS}
        self.cnt = {e: 0 for e in ENGS}
        self.seen = {e: {} for e in ENGS}
        self.dsems = [nc.alloc_semaphore('d%d' % i) for i in range(N_DMA_SEMS)]
        self.dcnt = {s.num: 0 for s in self.dsems}
        self.dfree = list(self.dsems)
        self.swsems = []
        self.semobj = {s.num: s for s in list(self.sem.values()) + self.dsems}
        self.sb_off = None
        self.sb_base = None
        self.uid = 0
        self.banks = []
        self.pairs = [nc.alloc_psum_tensor('psp%d' % i, [128, 2, 512], F32) for i in range(4)]
        for i in range(8):
            self.banks.append(Tile(BankView(self.pairs[i // 2], i % 2), name='ps%d' % i))
        self.bank_rr = 0
        self.rr_n = 8

    def arena(self, base, top):
        self.sb_base, self.sb_top = base, top
        self.sb_off = base

    def mark(self):
        return self.sb_off

    def release(self, mark):
        self.sb_off = mark

    def tile(self, shape, dtype, name=''):
        esz = 2 if dtype == BF16 else 4
        n = 1
        for s in shape[1:]:
            n *= s
        nbytes = (n * esz + 31) // 32 * 32
        off = (self.sb_off + 31) // 32 * 32
        assert off + nbytes <= self.sb_top, 'SBUF overflow %s need %d at %d top %d' % (name, nbytes, off, self.sb_top)
        self.uid += 1
        t = self.nc.alloc_sbuf_tensor_at('%s_%d' % (name or 't', self.uid), list(shape), dtype, offset=off)
        self.sb_off = off + nbytes
        return Tile(t, name=name)

    def psum(self, i=None):
        if i is None:
            i = self.bank_rr
            i = i % self.rr_n
            self.bank_rr = (i + 1) % self.rr_n
        return self.banks[i]

    def _waits(self, eng, R, W):
        need = {}

        def add(c):
            if c is None:
                return
            s, v = c
            if need.get(s, 0) < v:
                need[s] = v
        for t in R:
            add(t.w)
        for t in W:
            add(t.w)
            for c in t.r:
                add(c)
        out = []
        own = self.sem[eng].num
        seen = self.seen[eng]
        for s, v in need.items():
            if s == own and eng == 'pe':
                continue
            if seen.get(s, 0) >= v:
                continue
            seen[s] = v
            out.append((self.semobj[s], v))
        return out

    def _commit(self, comp, R, W):
        for t in R:
            t.r.append(comp)
        for t in W:
            t.w = comp
            t.r = []

    def op(self, eng, fn, R=(), W=()):
        R = [x.tok if isinstance(x, Tile) else x for x in R]
        W = [x.tok if isinstance(x, Tile) else x for x in W]
        waits = self._waits(eng, R, W)
        self.cnt[eng] += 1
        comp = (self.sem[eng].num, self.cnt[eng])
        self.ops[eng].append((fn, waits, (self.sem[eng], 1)))
        self._commit(comp, R, W)
        return comp

    def dma(self, eng, out, in_, R=(), W=(), semtok=None):
        R = [x.tok if isinstance(x, Tile) else x for x in R]
        W = [x.tok if isinstance(x, Tile) else x for x in W]
        st = semtok or (W[0] if W else R[0])
        if isinstance(st, Tile):
            st = st.tok
        if eng == 'pool':
            ns = self.nc.alloc_semaphore('sw%d' % len(self.semobj))
            self.semobj[ns.num] = ns
            self.dcnt[ns.num] = 0
            self.swsems.append(ns)
            st.dsem = ns
        elif st.dsem is None or st.dsem in self.swsems:
            st.dsem = self.dfree.pop()
        waits = self._waits(eng, R, W)
        self.dcnt[st.dsem.num] += 16
        comp = (st.dsem.num, self.dcnt[st.dsem.num])
        self.ops[eng].append((lambda e: e.dma_start(out=out, in_=in_), waits, (st.dsem, 16)))
        self._commit(comp, R, W)
        return comp

    def barrier(self, recycle=True):
        for e in ENGS:
            waits = []
            seen = self.seen[e]
            for o in ENGS:
                s = self.sem[o]
                if self.cnt[o] > seen.get(s.num, 0):
                    seen[s.num] = self.cnt[o]
                    waits.append((s, self.cnt[o]))
            for s in self.dsems + self.swsems:
                v = self.dcnt[s.num]
                if v > seen.get(s.num, 0):
                    seen[s.num] = v
                    waits.append((s, v))
            if waits:
                self.ops[e].append((None, waits, None))
        if recycle:
            self.dfree = list(self.dsems)

    def emit(self):
        nc = self.nc
        ops = self.ops

        def run(name, eng):
            for fn, waits, inc in ops[name]:
                for s, v in waits:
                    eng.wait_ge(s, v)
                if fn is not None:
                    ins = fn(eng)
                    if inc is not None:
                        ins.then_inc(inc[0], inc[1])

        with nc.Block() as block:
            @block.tensor
            def _(e):
                run('pe', e)

            @block.scalar
            def _(e):
                run('act', e)

            @block.vector
            def _(e):
                run('dve', e)

            @block.gpsimd
            def _(e):
                run('pool', e)

            @block.sync
            def _(e):
                run('sp', e)

    def mm(self, out, lhsT, rhs, start, stop, R, W):
        return self.op('pe', lambda e: e.matmul(out, lhsT, rhs, start=start, stop=stop), R, W)

    def act(self, out, in_, func, R, W, bias=None, scale=None, accum_out=None, eng='act'):
        kw = {}
        if bias is not None:
            kw['bias'] = bias
        if scale is not None:
            kw['scale'] = scale
        if accum_out is not None:
            kw['accum_out'] = accum_out
        return self.op('act', lambda e: e.activation(out, in_, func, **kw), R, W)

    def tt(self, eng, out, a, b, op, R, W):
        return self.op(eng, lambda e: e.tensor_tensor(out, a, b, op), R, W)

    def ts(self, eng, out, a, s1, s2, op0, op1, R, W):
        if op1 is None:
            return self.op(eng, lambda e: e.tensor_scalar(out, a, s1, None, op0), R, W)
        return self.op(eng, lambda e: e.tensor_scalar(out, a, s1, s2, op0, op1), R, W)

    def stt(self, eng, out, a, s, b, op0, op1, R, W):
        return self.op(eng, lambda e: e.scalar_tensor_tensor(out, a, s, b, op0, op1), R, W)

    def copy(self, eng, out, in_, R, W):
        if eng == 'act':
            return self.op('act', lambda e: e.copy(out, in_), R, W)
        return self.op(eng, lambda e: e.tensor_copy(out, in_), R, W)

    def memset(self, eng, ap, val, W):
        return self.op(eng, lambda e: e.memset(ap, val), (), W)

    def recip(self, out, in_, R, W):
        return self.op('dve', lambda e: e.reciprocal(out, in_), R, W)

    def allgather(self, out, in_, groups, tok):
        ns = self.nc.alloc_semaphore('cc%d' % len(self.semobj))
        self.semobj[ns.num] = ns
        self.dcnt[ns.num] = 16
        self.swsems.append(ns)
        waits = self._waits('pool', [], [tok])
        comp = (ns.num, 16)
        self.ops['pool'].append((lambda e: e.collective_compute('AllGather', ALU.bypass, groups, [in_], [out]), waits, (ns, 16)))
        self._commit(comp, [], [tok])
        return comp

D = 1024
DFF = 4096
NCTX = 256
EPS = 1e-6
DEPTH = 4
E_Q, E_KV, E_R, E_U, EVEN_IN = 384, 640, 672, 1184, 1696
O_K, O_V, O_ZF, O_ZB, O_Q, ODD_IN = 512, 1536, 1552, 1568, 2080, 3104
MLA_SCALE = 96 ** -0.5
LV = 72
NV = DEPTH * LV + 24
SB_BASE, SB_TOP = 16512, 229344


class Env:
    def __init__(self, nc, fused):
        self.nc, self.fused = nc, fused
        self.t = {}
        self.ins, self.outs = {}, {}

    def r(self, name, shape, dtype):
        if name not in self.t:
            self.t[name] = self.nc.dram_tensor(name, list(shape), dtype, kind='ExternalInput').ap()
            self.ins[name] = (tuple(shape), dtype)
        return self.t[name]

    inp = r

    def w(self, name, shape, dtype, external=False):
        if name not in self.t:
            kind = 'ExternalOutput' if (external or not self.fused) else 'Internal'
            self.t[name] = self.nc.dram_tensor(name, list(shape), dtype, kind=kind).ap()
            if kind == 'ExternalOutput':
                self.outs[name] = (tuple(shape), dtype)
        return self.t[name]


class B:
    def __init__(self, nc, Lc, fused, G=4):
        self.nc = nc
        self.G = G
        self.Lc = Lc
        self.NT = Lc + NCTX
        self.env = Env(nc, fused)
        self.S = S = Sched(nc)
        S.arena(SB_BASE, SB_TOP)
        self.fused = fused
        self.vecs = S.tile([128, NV], F32, 'vecs')
        self.ones = S.tile([128, 128], BF16, 'ones')
        self.mod = S.tile([128, 48, 2], F32, 'mod')
        self.ab = S.tile([128, 4, 8, 2], F32, 'ab')
        self.sc = S.tile([128, 8, 2], F32, 'sc')
        S.memset('pool', self.ones[:], 1.0, W=[self.ones])
        v = self.env.inp('vecs', [128, NV], F32)
        S.dma('sp', self.vecs[:], v, W=[self.vecs])
        g = DEPTH * LV
        tmp = S.tile([128, 16], F32, 'sctmp')
        S.act(tmp[:], self.vecs[:, g:g + 16], AF.Sigmoid, R=[self.vecs], W=[tmp])
        S.tt('dve', self.sc[:, :, 0], tmp[:, 0:8], self.vecs[:, g:g + 8], ALU.mult, R=[tmp, self.vecs], W=[self.sc])
        S.tt('dve', self.sc[:, :, 1], tmp[:, 8:16], self.vecs[:, g + 8:g + 16], ALU.mult, R=[tmp, self.vecs], W=[self.sc])
        self.res_mark = S.mark()

    def tiles(self, T):
        r = [(o, min(T, self.Lc - o), 0) for o in range(0, self.Lc, T)]
        for o in range(0, NCTX, T):
            r.append((self.Lc + o, min(T, NCTX - o), 1))
        return r

    def wload(self, name, rows, cols, c0=0, c1=None, dtype=BF16, tname=None):
        S = self.S
        c1 = cols if c1 is None else c1
        w = self.env.inp(name, [rows, cols], F32)
        kc = rows // 128
        t = S.tile([128, kc, c1 - c0], dtype, tname or name)
        src = w.rearrange('(k p) c -> p k c', p=128)[:, :, c0:c1]
        S.dma('pool' if dtype != F32 else 'sp', t[:], src, W=[t])
        return t

    def adaln(self, i):
        S = self.S
        m = S.mark()
        aw = self.env.inp('ada_w%d' % i, [D, 6 * D], F32)
        src = aw.rearrange('(k p) c -> p k c', p=128)
        wts = [S.tile([128, 8, 1024], F32, 'adaw%d' % q) for q in range(2)]
        for piece in range(6):
            wt = wts[piece % 2]
            S.dma('sp', wt[:], src[:, :, piece * 1024:(piece + 1) * 1024], W=[wt])
            ps = S.psum()
            for f in range(8):
                for k in range(8):
                    S.mm(ps[:, f * 2:f * 2 + 2], wt[:, k, f * 128:(f + 1) * 128], self.sc[:, k, :],
                         k == 0, k == 7, R=[wt, self.sc], W=[ps])
            for j in range(2):
                S.tt('dve', self.mod[:, piece * 8:(piece + 1) * 8, j],
                     ps[:, 0:16].rearrange('p (f j) -> p f j', j=2)[:, :, j],
                     self.vecs[:, i * LV + piece * 8:i * LV + piece * 8 + 8], ALU.add,
                     R=[ps, self.vecs], W=[self.mod])
        for j in range(2):
            for n, (gcol, scale_p, shift_p) in enumerate(((48, 1, 0), (56, 4, 3))):
                S.stt('dve', self.ab[:, 2 * n, :, j], self.mod[:, scale_p * 8:scale_p * 8 + 8, j], 1.0,
                      self.vecs[:, i * LV + gcol:i * LV + gcol + 8], ALU.add, ALU.mult,
                      R=[self.mod, self.vecs], W=[self.ab])
                S.copy('dve', self.ab[:, 2 * n + 1, :, j], self.mod[:, shift_p * 8:shift_p * 8 + 8, j],
                       R=[self.mod], W=[self.ab])
        S.barrier()
        S.release(m)

    def rstd_of(self, chunks, Rtoks, T, dim, sq_tile, rstd_tile, sq_eng='pool'):
        S = self.S
        n = len(chunks)
        for k, c in enumerate(chunks):
            if sq_eng == 'act':
                S.act(sq_tile[:, k, :T], c, AF.Square, R=Rtoks, W=[sq_tile])
            else:
                S.tt(sq_eng, sq_tile[:, k, :T], c, c, ALU.mult, R=Rtoks, W=[sq_tile])
        ps = S.psum()
        for k in range(n):
            S.mm(ps[:, :T], self.ones[:], sq_tile[:, k, :T], k == 0, k == n - 1, R=[self.ones, sq_tile], W=[ps])
        S.act(rstd_tile[:, :T], ps[:, :T], AF.Sqrt, R=[ps], W=[rstd_tile], scale=1.0 / dim, bias=EPS)
        S.recip(rstd_tile[:, :T], rstd_tile[:, :T], R=[rstd_tile], W=[rstd_tile])

    def norm_mod(self, xt, T, which, j, hT, sq, rstd, tmp):
        S = self.S
        self.rstd_of([xt[:, k, :T] for k in range(8)], [xt], T, D, sq, rstd)
        for k in range(8):
            tp = tmp[k % 2]
            S.tt('dve', tp[:, :T], xt[:, k, :T], rstd[:, :T], ALU.mult, R=[xt, rstd], W=[tp])
            S.act(hT[:, k, :T], tp[:, :T], AF.Identity, R=[tp, self.ab], W=[hT],
                  scale=self.ab[:, 2 * which, k, j:j + 1], bias=self.ab[:, 2 * which + 1, k, j:j + 1])

    def work_tiles(self, T, tmp=True):
        S = self.S
        w = {}
        w['xt'] = S.tile([128, 8, T], F32, 'xt')
        w['hT'] = S.tile([128, 8, T], BF16, 'hT')
        w['sq'] = S.tile([128, 8, T], BF16, 'sq')
        w['rstd'] = S.tile([128, T], F32, 'rstd')
        if tmp:
            w['tmp'] = [S.tile([128, T], F32, 'tmp%d' % q) for q in range(2)]
        w['xt2'] = S.tile([128, 8, T], F32, 'xt2')
        return w

    def xiter(self, tl, xs, w, look=False):
        S = self.S
        xts = [w['xt'], w['xt2']]

        def load(n):
            off, tw, j = tl[n]
            S.dma('sp', xts[n % 2][:, :, :tw], xs[:, :, off:off + tw], W=[xts[n % 2]])
        if tl:
            load(0)
        for n, t in enumerate(tl):
            if n + 1 < len(tl):
                load(n + 1)
            if look:
                yield xts[n % 2], t, ((xts[(n + 1) % 2], tl[n + 1]) if n + 1 < len(tl) else None)
            else:
                yield xts[n % 2], t

    def xs_view(self, ap):
        return ap.rearrange('(k p) t -> p k t', p=128)

    def mlp(self, w, T, j, w1, w2, h1, rl):
        S = self.S
        hT, xt = w['hT'], w['xt']
        for half in range(2):
            for fl in range(16):
                f = half * 16 + fl
                ps = S.psum()
                for k in range(8):
                    S.mm(ps[:, :T], w1[:, k, f * 128:(f + 1) * 128], hT[:, k, :T], k == 0, k == 7, R=[w1, hT], W=[ps])
                r = rl[f % 2]
                S.act(r[:, :T], ps[:, :T], AF.Relu, R=[ps], W=[r])
                S.tt('pool' if f % 2 else 'dve', h1[:, fl, :T], r[:, :T], r[:, :T], ALU.mult, R=[r], W=[h1.toks[fl]])
            for o in range(8):
                ps = S.psum()
                for fl in range(16):
                    f = half * 16 + fl
                    S.mm(ps[:, :T], w2[:, f, o * 128:(o + 1) * 128], h1[:, fl, :T], fl == 0, fl == 15, R=[w2, h1.toks[fl]], W=[ps])
                S.stt('dve', xt[:, o, :T], ps[:, :T], self.mod[:, 40 + o, j:j + 1], xt[:, o, :T], ALU.mult, ALU.add,
                      R=[ps, self.mod, xt], W=[xt])

    def phase_M(self, i, final):
        S = self.S
        env = self.env
        NT = self.NT
        T = 512
        m = S.mark()
        xs = self.xs_view(env.w('xs', [D, NT], F32))
        w1 = self.wload('mlp_w1_%d' % i, D, DFF)
        w2 = self.wload('mlp_w2_%d' % i, DFF, D)
        w = self.work_tiles(T, tmp=False)
        h1 = S.tile([128, 16, T], BF16, 'h1')
        h1.toks = [Tok() for _ in range(16)]
        rl = [S.tile([128, T], F32, 'rl%d' % q) for q in range(2)]
        w['tmp'] = rl
        if final:
            fin = env.w('yT', [D, self.Lc], F32, external=True).rearrange('(k p) t -> p k t', p=128)
            gF = DEPTH * LV + 16
        for xt, (off, tw, j) in self.xiter([t for t in self.tiles(T) if not (final and t[2] == 1)], xs, w):
            w['xt'] = xt
            self.norm_mod(xt, tw, 1, j, w['hT'], w['sq'], w['rstd'], w['tmp'])
            self.mlp(w, tw, j, w1, w2, h1, rl)
            if final:
                self.rstd_of([xt[:, k, :tw] for k in range(8)], [xt], tw, D, w['sq'], w['rstd'])
                for k in range(8):
                    S.stt('dve', xt[:, k, :tw], xt[:, k, :tw], self.vecs[:, gF + k:gF + k + 1], w['rstd'][:, :tw],
                          ALU.mult, ALU.mult, R=[xt, self.vecs, w['rstd']], W=[xt])
                S.dma('sp', fin[:, :, off:off + tw], xt[:, :, :tw], R=[xt])
            else:
                S.dma('sp', xs[:, :, off:off + tw], xt[:, :, :tw], R=[xt])
        S.barrier()
        S.release(m)

    def phase_E3(self, i):
        S = self.S
        env = self.env
        NT = self.NT
        T = 512
        m = S.mark()
        xs = self.xs_view(env.w('xs', [D, NT], F32))
        wo = self.wload('ev_w_out%d' % (i // 2), D, D)
        ats = env.r('ATs', [512, NT], BF16).rearrange('(k p) t -> p k t', p=128)
        mls = env.r('MLs', [512, NT], BF16).rearrange('(k p) t -> p k t', p=128)
        xts = [S.tile([128, 8, T], F32, 'xt%d' % q) for q in range(2)]
        mixs = [S.tile([128, 8, T], BF16, 'mix%d' % q) for q in range(2)]
        for n, (off, tw, j) in enumerate(self.tiles(T)):
            xt, mix = xts[n % 2], mixs[n % 2]
            S.dma('sp', xt[:, :, :tw], xs[:, :, off:off + tw], W=[xt])
            S.dma('sp', mix[:, 0:4, :tw], ats[:, :, off:off + tw], W=[mix])
            S.dma('sp', mix[:, 4:8, :tw], mls[:, :, off:off + tw], W=[mix])
            for o in range(8):
                ps = S.psum()
                for k in range(8):
                    S.mm(ps[:, :tw], wo[:, k, o * 128:(o + 1) * 128], mix[:, k, :tw], k == 0, k == 7, R=[wo, mix], W=[ps])
                S.stt('dve', xt[:, o, :tw], ps[:, :tw], self.mod[:, 16 + o, j:j + 1], xt[:, o, :tw], ALU.mult, ALU.add,
                      R=[ps, self.mod, xt], W=[xt])
            S.dma('sp', xs[:, :, off:off + tw], xt[:, :, :tw], R=[xt])
        S.barrier()
        S.release(m)

    def phase_E1(self, i):
        S = self.S
        env = self.env
        NT = self.NT
        T = 512
        jx = i // 2
        m = S.mark()
        vb = i * LV + 64
        xs = self.xs_view(env.r('xs', [D, NT], F32))
        Qs = env.w('Qs', [96, 8, NT], BF16)
        LATs = env.w('LATs', [256, NT], BF16).rearrange('(k p) t -> p k t', p=128)
        KRs = env.w('KRs', [32, NT], BF16)
        MLs = env.w('MLs', [512, NT], BF16).rearrange('(k p) t -> p k t', p=128)
        ropeQ = env.inp('ropeQ', [96, 2, NT], F32)
        ropeK = env.inp('ropeK', [32, 2, NT], F32)
        win = self.wload('ev_w_in%d' % jx, D, EVEN_IN)
        wkp = self.wload('ev_w_in_kp%d' % jx, D, 32)
        wuq = self.wload('ev_w_uq%d' % jx, 384, 768)
        wuqp = self.wload('ev_w_uq_p%d' % jx, 384, 768)
        wsT = S.tile([128, 4, 128], BF16, 'wsT')
        S.dma('pool', wsT[:], env.inp('ev_wsT%d' % jx, [128, 4, 128], F32), W=[wsT])
        bc = S.tile([128, 640], F32, 'bcE')
        S.dma('sp', bc[:], env.inp('ev_bc%d' % jx, [128, 640], F32), W=[bc])
        w = self.work_tiles(T)
        xt, hT, sq, rstd = w['xt'], w['hT'], w['sq'], w['rstd']
        tq = S.tile([96, 2, T], F32, 'tq')
        tk = S.tile([32, 2, T], F32, 'tk')
        cqn = S.tile([128, 3, T], BF16, 'cqn')
        qst = S.tile([96, 8, T], BF16, 'qst')
        qtmp = [S.tile([96, 2, T], F32, 'qtmp%d' % q) for q in range(2)]
        lat = S.tile([128, 2, T], BF16, 'lat')
        kr = S.tile([32, T], BF16, 'kr')
        u = S.tile([128, 4, T], F32, 'u')
        gv = S.tile([128, 512], F32, 'gv')
        junk = S.tile([128, 128], BF16, 'junk')
        ss = S.tile([128, 8], F32, 'ss')
        vn = S.tile([128, 4, 128], BF16, 'vn')
        t2 = S.tile([128, 512], F32, 't2')
        ml = S.tile([128, 4, T], BF16, 'ml')
        for xt, (off, tw, j) in self.xiter(self.tiles(T), xs, w):
            S.dma('sp', tq[:, :, :tw], ropeQ[:, :, off:off + tw], W=[tq])
            S.dma('sp', tk[:, :, :tw], ropeK[:, :, off:off + tw], W=[tk])
            self.norm_mod(xt, tw, 0, j, hT, sq, rstd, w['tmp'])
            pcs = []
            for c in range(3):
                ps = S.psum()
                for k in range(8):
                    S.mm(ps[:, :tw], win[:, k, c * 128:(c + 1) * 128], hT[:, k, :tw], k == 0, k == 7, R=[win, hT], W=[ps])
                pcs.append(ps)
            self.rstd_of([p[:, :tw] for p in pcs], pcs, tw, 384, sq, rstd, sq_eng='act')
            for c in range(3):
                S.stt('dve', cqn[:, c, :tw], pcs[c][:, :tw], self.vecs[:, vb + c:vb + c + 1], rstd[:, :tw], ALU.mult, ALU.mult,
                      R=[pcs[c], self.vecs, rstd], W=[cqn])
            pcs = []
            for c in range(2):
                ps = S.psum()
                for k in range(8):
                    S.mm(ps[:, :tw], win[:, k, E_Q + c * 128:E_Q + (c + 1) * 128], hT[:, k, :tw], k == 0, k == 7, R=[win, hT], W=[ps])
                pcs.append(ps)
            self.rstd_of([p[:, :tw] for p in pcs], pcs, tw, 256, sq, rstd, sq_eng='act')
            for c in range(2):
                S.stt('dve', lat[:, c, :tw], pcs[c][:, :tw], self.vecs[:, vb + 3 + c:vb + 4 + c], rstd[:, :tw], ALU.mult, ALU.mult,
                      R=[pcs[c], self.vecs, rstd], W=[lat])
            S.dma('sp', LATs[:, :, off:off + tw], lat[:, :, :tw], R=[lat])
            pa, pb = S.psum(), S.psum()
            for k in range(8):
                S.mm(pa[0:32, :tw], win[:, k, E_KV:E_R], hT[:, k, :tw], k == 0, k == 7, R=[win, hT], W=[pa])
            for k in range(8):
                S.mm(pb[0:32, :tw], wkp[:, k, :], hT[:, k, :tw], k == 0, k == 7, R=[wkp, hT], W=[pb])
            qt = qtmp[0]
            S.tt('dve', qt[0:32, 0, :tw], pa[0:32, :tw], tk[:, 0, :tw], ALU.mult, R=[pa, tk], W=[qt])
            S.tt('dve', qt[0:32, 1, :tw], pb[0:32, :tw], tk[:, 1, :tw], ALU.mult, R=[pb, tk], W=[qt])
            S.tt('pool', kr[:, :tw], qt[0:32, 0, :tw], qt[0:32, 1, :tw], ALU.add, R=[qt], W=[kr])
            S.dma('sp', KRs[:, off:off + tw], kr[:, :tw], R=[kr])
            for g in range(4):
                ps = S.psum()
                for k in range(8):
                    S.mm(ps[:, :tw], win[:, k, E_R + g * 128:E_R + (g + 1) * 128], hT[:, k, :tw], k == 0, k == 7, R=[win, hT], W=[ps])
                S.act(u[:, g, :tw], ps[:, :tw], AF.Gelu_apprx_tanh, R=[ps], W=[u])
            for cc in range(tw // 128):
                ps = S.psum()
                for k in range(8):
                    S.mm(ps[:, :], hT[:, k, cc * 128:(cc + 1) * 128], win[:, k, E_U:EVEN_IN], k == 0, k == 7, R=[win, hT], W=[ps])
                S.act(gv[:], ps[:], AF.Gelu_apprx_tanh, R=[ps], W=[gv])
                for g in range(4):
                    S.act(junk[:], gv[:, g * 128:(g + 1) * 128], AF.Square, R=[gv], W=[junk, ss], accum_out=ss[:, g:g + 1])
                S.act(ss[:, 4:8], ss[:, 0:4], AF.Sqrt, R=[ss], W=[ss], scale=1.0 / 128, bias=EPS)
                S.recip(ss[:, 4:8], ss[:, 4:8], R=[ss], W=[ss])
                for g in range(4):
                    S.stt('dve', vn[:, g, :], gv[:, g * 128:(g + 1) * 128], ss[:, 4 + g:5 + g], bc[:, 0:128], ALU.mult, ALU.mult,
                          R=[gv, ss, bc], W=[vn])
                po = S.psum()
                for g in range(4):
                    S.mm(po[:, g * 128:(g + 1) * 128], vn[:, g, :], wsT[:, g, :], True, True, R=[vn, wsT], W=[po])
                S.tt('dve', t2[:], po[:], bc[:, 128:640], ALU.add, R=[po, bc], W=[t2])
                S.tt('pool', ml[:, :, cc * 128:(cc + 1) * 128], t2[:].rearrange('p (g t) -> p g t', g=4),
                     u[:, :, cc * 128:(cc + 1) * 128], ALU.mult, R=[t2, u], W=[ml])
            for h in range(8):
                pa, pb = S.psum(), S.psum()
                for c in range(3):
                    S.mm(pa[0:96, :tw], wuq[:, c, h * 96:(h + 1) * 96], cqn[:, c, :tw], c == 0, c == 2, R=[wuq, cqn], W=[pa])
                for c in range(3):
                    S.mm(pb[0:96, :tw], wuqp[:, c, h * 96:(h + 1) * 96], cqn[:, c, :tw], c == 0, c == 2, R=[wuqp, cqn], W=[pb])
                qt = qtmp[h % 2]
                S.tt('dve', qt[:, 0, :tw], pa[0:96, :tw], tq[:, 0, :tw], ALU.mult, R=[pa, tq], W=[qt])
                S.tt('dve', qt[:, 1, :tw], pb[0:96, :tw], tq[:, 1, :tw], ALU.mult, R=[pb, tq], W=[qt])
                S.tt('pool', qst[:, h, :tw], qt[:, 0, :tw], qt[:, 1, :tw], ALU.add, R=[qt], W=[qst])
            S.dma('sp', Qs[:, :, off:off + tw], qst[:, :, :tw], R=[qst])
            S.dma('sp', MLs[:, :, off:off + tw], ml[:, :, :tw], R=[ml])
        S.barrier()
        S.release(m)

    def phase_E2(self, i):
        S = self.S
        env = self.env
        NT = self.NT
        NKEY = self.G * self.Lc + NCTX
        NKB = NKEY // 128
        T = 512
        jx = i // 2
        m = S.mark()
        Qs = env.r('Qs', [96, 8, NT], BF16)
        ATs = env.w('ATs', [512, NT], BF16)
        LATall = env.r('LATall' if self.G > 1 else 'LATs', [256, NKEY], BF16).rearrange('(k p) t -> p k t', p=128)
        KRall = env.r('KRall' if self.G > 1 else 'KRs', [32, NKEY], BF16)
        wukv = self.wload('ev_w_ukv%d' % jx, 256, 1024)
        lat = S.tile([128, 2, NKEY], BF16, 'latall')
        KT = S.tile([96, NKEY], BF16, 'KT')
        KTr = Tok('KTr')
        VA = S.tile([128, NKB, 128], BF16, 'VA')
        qTs = [S.tile([96, T], BF16, 'qT%d' % q) for q in range(2)]
        PTs = [S.tile([128, 2, T], BF16, 'PT%d' % q) for q in range(3)]
        rec = S.tile([64, T], F32, 'rec')
        ast = [S.tile([64, T], BF16, 'ast%d' % q) for q in range(2)]
        for c in range(2):
            S.dma('sp', lat[:, c, :], LATall[:, c, :], W=[lat])
        S.dma('sp', KT[64:96, :], KRall, W=[KTr], semtok=KTr)
        S.memset('pool', VA[:, :, 64:128], 1.0, W=[VA])
        qn = 0
        for h in range(8):
            for n, k0 in enumerate(range(0, NKEY, 512)):
                kn = min(512, NKEY - k0)
                ps = S.psum(n % 6)
                for c in range(2):
                    S.mm(ps[0:64, :kn], wukv[:, c, h * 128:h * 128 + 64], lat[:, c, k0:k0 + kn], c == 0, c == 1, R=[wukv, lat], W=[ps])
                S.copy('dve', KT[0:64, k0:k0 + kn], ps[0:64, :kn], R=[ps], W=[KT])
            for n, b0 in enumerate(range(0, NKB, 8)):
                nb = min(8, NKB - b0)
                ps = S.psum(n % 6)
                for b in range(nb):
                    for c in range(2):
                        S.mm(ps[:, b * 64:(b + 1) * 64], lat[:, c, (b0 + b) * 128:(b0 + b + 1) * 128],
                             wukv[:, c, h * 128 + 64:h * 128 + 128], c == 0, c == 1, R=[wukv, lat], W=[ps])
                S.copy('dve', VA[:, b0:b0 + nb, 0:64],
                       ps[:, 0:nb * 64].rearrange('p (b d) -> p b d', d=64), R=[ps], W=[VA])
            for (off, tw, j) in self.tiles(T):
                blocks = list(range(NKB)) if j == 0 else list(range(NKB - NCTX // 128, NKB))
                qT = qTs[qn % 2]
                psO = S.psum(6 + qn % 2)
                a_st = ast[qn % 2]
                qn += 1
                S.dma('sp', qT[:, :tw], Qs[:, h, off:off + tw], W=[qT])
                npair = len(blocks) // 2
                LA = 2

                def qk(pi):
                    for u in range(2):
                        kb = blocks[2 * pi + u]
                        bk = S.banks[(pi % 3) * 2 + u]
                        S.mm(bk[:, :tw], KT[0:96, kb * 128:(kb + 1) * 128], qT[:, :tw], True, True, R=[KT, KTr, qT], W=[bk])
                for pi in range(min(LA, npair)):
                    qk(pi)
                for pi in range(npair):
                    PT = PTs[pi % 3]
                    pr = S.pairs[pi % 3]
                    b0, b1 = S.banks[(pi % 3) * 2], S.banks[(pi % 3) * 2 + 1]
                    S.act(PT[:, :, :tw], pr[:, :, :tw], AF.Exp, R=[b0, b1], W=[PT])
                    if pi + LA < npair:
                        qk(pi + LA)
                    for u in range(2):
                        S.mm(psO[:, :tw], VA[:, blocks[2 * pi + u], :], PT[:, u, :tw], pi == 0 and u == 0,
                             pi == npair - 1 and u == 1, R=[VA, PT], W=[psO])
                S.recip(rec[:, :tw], psO[64:128, :tw], R=[psO], W=[rec])
                S.tt('dve', a_st[:, :tw], psO[0:64, :tw], rec[:, :tw], ALU.mult, R=[psO, rec], W=[a_st])
                S.dma('sp', ATs[h * 64:(h + 1) * 64, off:off + tw], a_st[:, :tw], R=[a_st])
        S.barrier()
        S.release(m)

    def gla_consts(self):
        S = self.S
        cst = S.tile([128, 6, 128], F32, 'glacst')
        S.dma('sp', cst[:], self.env.r('gla_cst', [128, 6, 128], F32), W=[cst])
        return cst

    def gla_gate_tiles(self, jx, T):
        S = self.S
        g = {}
        for d in 'fb':
            t = S.tile([17, 512], F32, 'wg' + d)
            S.dma('sp', t[:], self.env.r('od_wg%s%d' % (d, jx), [17, 512], F32), W=[t])
            g['wg' + d] = t
            z = S.tile([17, T], F32, 'za' + d)
            S.memset('pool', z[:], 1.0, W=[z])
            g['za' + d] = z
        g['etmp'] = S.tile([128, 512], F32, 'etmp')
        g['Ee'] = S.tile([128, 512], F32, 'Ee')
        g['kend'] = S.tile([128, 512], BF16, 'kend')
        g['c'] = []
        for p in range(2):
            c = {}
            c['lf'] = S.tile([128, 512], F32, 'lf%d' % p)
            c['lb'] = S.tile([128, 512], F32, 'lb%d' % p)
            c['vbf'] = S.tile([128, 1024], BF16, 'vbf%d' % p)
            c['dec'] = S.tile([128, 8], F32, 'dec%d' % p)
            g['c'].append(c)
        return g

    def gla_z(self, g, win, hT, tw):
        S = self.S
        for n, d in enumerate('fb'):
            ps = S.psum()
            c0 = O_V + 16 * n
            for k in range(8):
                S.mm(ps[0:16, :tw], win[:, k, c0:c0 + 16], hT[:, k, :tw], k == 0, k == 7, R=[win, hT], W=[ps])
            S.copy('dve', g['za' + d][0:16, :tw], ps[0:16, :tw], R=[ps], W=[g['za' + d]])

    def gla_tokmajor(self, g, gc, win, hT, c0, bank):
        S = self.S
        psk = S.psum(bank)
        for k in range(8):
            S.mm(psk[:], hT[:, k, c0:c0 + 128], win[:, k, 0:O_K], k == 0, k == 7, R=[win, hT], W=[psk])
        for q in range(2):
            ps = S.psum()
            for k in range(8):
                S.mm(ps[:], hT[:, k, c0:c0 + 128], win[:, k, O_K + q * 512:O_K + (q + 1) * 512], k == 0, k == 7, R=[win, hT], W=[ps])
            S.copy('act' if q else 'dve', gc['vbf'][:, q * 512:(q + 1) * 512], ps[:], R=[ps], W=[gc['vbf']])
        for d in 'fb':
            ps = S.psum()
            S.mm(ps[:], g['za' + d][0:17, c0:c0 + 128], g['wg' + d][0:17, :], True, True, R=[g['za' + d], g['wg' + d]], W=[ps])
            S.act(g['etmp'][:], ps[:], AF.Exp, R=[ps], W=[g['etmp']], scale=-1.0)
            S.act(gc['l' + d][:], g['etmp'][:], AF.Ln, R=[g['etmp']], W=[gc['l' + d]], bias=1.0)
        ps = S.psum()
        for n, d in enumerate('fb'):
            for h in range(4):
                S.mm(ps[:, n * 4 + h:n * 4 + h + 1], gc['l' + d][:, h * 128:(h + 1) * 128], self.cst[:, 0, 127:128], True, True,
                     R=[gc['l' + d], self.cst], W=[ps])
        S.act(gc['dec'][:], ps[:, 0:8], AF.Exp, R=[ps], W=[gc['dec']])
        return psk

    def gla_dS(self, g, gc, psk, d):
        S = self.S
        ps = S.psum()
        S.mm(ps[:], self.cst[:, 2 if d == 'f' else 3, :], gc['l' + d][:], True, True, R=[self.cst, gc['l' + d]], W=[ps])
        S.act(g['Ee'][:], ps[:], AF.Exp, R=[ps], W=[g['Ee']])
        S.tt('dve', g['kend'][:], psk[:], g['Ee'][:], ALU.mult, R=[psk, g['Ee']], W=[g['kend']])
        banks = []
        for q in range(2):
            pd = S.psum()
            for hh in range(2):
                h = q * 2 + hh
                S.mm(pd[:, hh * 256:(hh + 1) * 256], g['kend'][:, h * 128:(h + 1) * 128], gc['vbf'][:, h * 256:(h + 1) * 256],
                     True, True, R=[g['kend'], gc['vbf']], W=[pd])
            banks.append(pd)
        return banks

    def phase_O1(self, i):
        S = self.S
        env = self.env
        NT, Lc = self.NT, self.Lc
        T = 256
        jx = i // 2
        NCH = NT // 128
        m = S.mark()
        S.rr_n = 6
        xs = self.xs_view(env.r('xs', [D, NT], F32))
        SBs = env.w('SBs', [128, NCH, 4, 256], BF16)
        DCs = env.w('DCs', [128, NCH, 4], F32)
        EXs = env.w('EXs', [128, 2, 4, 257], F32)
        CXs = env.w('CXs', [128, 2, 4, 256], F32)
        win = self.wload('od_w_in%d' % jx, D, ODD_IN, 0, O_ZB)
        self.cst = self.gla_consts()
        g = self.gla_gate_tiles(jx, T)
        w = self.work_tiles(T)
        xt, hT = w['xt'], w['hT']
        hTs = [hT, S.tile([128, 8, T], BF16, 'hTb')]
        dcum = S.tile([128, NCH, 4], F32, 'dcum')
        sbst = [S.tile([128, 4, 256], BF16, 'sbst%d' % q) for q in range(2)]
        exs = S.tile([128, 2, 4, 257], F32, 'exs')
        nst = 0
        for chain in (1, 0):
            S.memset('pool', exs[:], 0.0, W=[exs])
            S.memset('pool', exs[:, :, :, 256:257], 1.0, W=[exs])
            tl = [t for t in self.tiles(T) if t[2] == chain]
            for n_, (xt, (off, tw, j), nxt) in enumerate(self.xiter(list(reversed(tl)), xs, w, look=True)):
                hT = hTs[n_ % 2]
                if n_ == 0:
                    self.norm_mod(xt, tw, 0, j, hT, w['sq'], w['rstd'], w['tmp'])
                self.gla_z(g, win, hT, tw)
                chunks = list(reversed(range(tw // 128)))
                psks = [self.gla_tokmajor(g, g['c'][ci], win, hT, cc * 128, 7 - ci) for ci, cc in enumerate(chunks)]
                if nxt is not None:
                    self.norm_mod(nxt[0], nxt[1][1], 0, nxt[1][2], hTs[(n_ + 1) % 2], w['sq'], w['rstd'], w['tmp'])
                for ci, cc in enumerate(chunks):
                    gc, psk = g['c'][ci], psks[ci]
                    n = (off + cc * 128) // 128
                    st = sbst[nst % 2]
                    nst += 1
                    S.copy('pool', st[:], exs[:, 1, :, 0:256], R=[exs], W=[st])
                    S.dma('sp', SBs[:, n], st[:], R=[st])
                    S.copy('pool', dcum[:, n, :], exs[:, 1, :, 256], R=[exs], W=[dcum])
                    bk = self.gla_dS(g, gc, psk, 'b')
                    for h in range(4):
                        S.stt('dve', exs[:, 1, h, 0:256], exs[:, 1, h, 0:256], gc['dec'][:, 4 + h:5 + h],
                              bk[h // 2][:, (h % 2) * 256:(h % 2 + 1) * 256], ALU.mult, ALU.add, R=[exs, gc['dec'], bk[h // 2]], W=[exs])
                    S.tt('dve', exs[:, 1, :, 256], exs[:, 1, :, 256], gc['dec'][:, 4:8], ALU.mult, R=[exs, gc['dec']], W=[exs])
                    bk = self.gla_dS(g, gc, psk, 'f')
                    for h in range(4):
                        S.stt('dve', exs[:, 0, h, 0:256], bk[h // 2][:, (h % 2) * 256:(h % 2 + 1) * 256], exs[:, 0, h, 256:257],
                              exs[:, 0, h, 0:256], ALU.mult, ALU.add, R=[exs, bk[h // 2]], W=[exs])
                    S.tt('dve', exs[:, 0, :, 256], exs[:, 0, :, 256], gc['dec'][:, 0:4], ALU.mult, R=[exs, gc['dec']], W=[exs])
            if chain == 1:
                S.dma('sp', CXs, exs[:, :, :, 0:256], R=[exs])
            else:
                S.dma('sp', EXs, exs[:], R=[exs])
        S.dma('sp', DCs, dcum[:], R=[dcum])
        S.barrier()
        S.rr_n = 8
        S.release(m)

    def phase_O2(self, i):
        S = self.S
        env = self.env
        NT, Lc = self.NT, self.Lc
        T = 256
        jx = i // 2
        NCH = NT // 128
        LNQ = float(np.log(128.0 ** -0.5))
        m = S.mark()
        S.rr_n = 6
        vb = i * LV + 64
        xs = self.xs_view(env.w('xs', [D, NT], F32))
        SBs = env.r('SBs', [128, NCH, 4, 256], BF16)
        DCs = env.r('DCs', [128, NCH, 4], F32)
        CXs = env.r('CXs', [128, 2, 4, 256], F32)
        if self.G > 1:
            EXall = env.r('EXall', [128, 4, 2, 4, 257], F32)
        pmk = env.r('posmask', [128, 16], F32)
        win = self.wload('od_w_in%d' % jx, D, ODD_IN)
        wo = self.wload('od_w_out%d' % jx, D, D)
        self.cst = cst = self.gla_consts()
        g = self.gla_gate_tiles(jx, T)
        w = self.work_tiles(T)
        xt, hT = w['xt'], w['hT']
        hTs = [hT, S.tile([128, 8, T], BF16, 'hTb')]
        mk4 = S.tile([128, 2, 4, 128], F32, 'mk4')
        for q in range(2):
            for h in range(4):
                S.copy('pool', mk4[:, q, h, :], cst[:, 4 + q, :], R=[cst], W=[mk4])
        dcum = S.tile([128, NCH, 4], F32, 'dcum')
        S.dma('sp', dcum[:], DCs, W=[dcum])
        pm = S.tile([128, 16], F32, 'pm')
        S.dma('sp', pm[:], pmk, W=[pm])
        Sin = S.tile([128, 2, 4, 256], F32, 'Sin')
        S.dma('sp', Sin[:], CXs, W=[Sin])
        ex = S.tile([128, 2, 4, 257], F32, 'ex')
        dp = S.tile([128, 4], F32, 'dp')
        sft = S.tile([128, 256], F32, 'sft')
        for d, order in (((0, range(4)), (1, reversed(range(4)))) if self.G > 1 else ()):
            for q in order:
                S.dma('sp', ex[:], EXall[:, q], W=[ex])
                mcol = pm[:, d * 4 + q:d * 4 + q + 1]
                S.ts('dve', dp[:], ex[:, d, :, 256], mcol, None, ALU.mult, None, R=[ex, pm], W=[dp])
                S.ts('dve', dp[:], dp[:], pm[:, 8 + d * 4 + q:9 + d * 4 + q], None, ALU.add, None, R=[dp, pm], W=[dp])
                for h in range(4):
                    S.ts('dve', sft[:], ex[:, d, h, 0:256], mcol, None, ALU.mult, None, R=[ex, pm], W=[sft])
                    S.stt('dve', Sin[:, d, h, :], Sin[:, d, h, :], dp[:, h:h + 1], sft[:], ALU.mult, ALU.add, R=[Sin, dp, sft], W=[Sin])
        Sf = S.tile([128, 4, 256], F32, 'Sf')
        Sfb = S.tile([128, 4, 256], BF16, 'Sfb')
        Sbl = S.tile([128, 4, 256], BF16, 'Sbl')
        Sbe = S.tile([128, 4, 256], BF16, 'Sbe')
        kTs = S.tile([128, 4, T], F32, 'kTs')
        qTs = S.tile([128, 4, T], F32, 'qTs')
        rs = S.tile([128, 2, 4, T], BF16, 'rs')
        mixT = S.tile([128, 2, 4, T], BF16, 'mixT')
        E4 = [S.tile([128, 4, 128], F32, 'E4_%d' % q) for q in range(4)]
        qk4s = [[S.tile([128, 4, 128], BF16, 'qk4_%d_%d' % (p, q)) for q in range(4)] for p in range(2)]
        A1 = S.tile([128, 4, 128], F32, 'A1')
        A2 = S.tile([128, 4, 128], F32, 'A2')
        Ab = S.tile([128, 4, 128], BF16, 'Ab')
        osq = [S.tile([128, 512], BF16, 'osq%d' % q) for q in range(2)]
        rso = S.tile([128, 4, 128], F32, 'rso')
        mtmp = S.tile([128, 4, 128], F32, 'mtmp')
        for chain in (1, 0):
            if chain == 1:
                S.memset('pool', Sf[:], 0.0, W=[Sf])
            else:
                S.copy('pool', Sf[:], Sin[:, 0], R=[Sin], W=[Sf])
            S.copy('pool', Sfb[:], Sf[:], R=[Sf], W=[Sfb])
            for n_, (xt, (off, tw, j), nxt) in enumerate(self.xiter([t for t in self.tiles(T) if t[2] == chain], xs, w, look=True)):
                hT = hTs[n_ % 2]
                if n_ == 0:
                    self.norm_mod(xt, tw, 0, j, hT, w['sq'], w['rstd'], w['tmp'])
                self.gla_z(g, win, hT, tw)
                for h in range(4):
                    for dst, c0 in ((kTs, 0), (qTs, O_ZB)):
                        ps = S.psum()
                        for k in range(8):
                            S.mm(ps[:, :tw], win[:, k, c0 + h * 128:c0 + (h + 1) * 128], hT[:, k, :tw], k == 0, k == 7, R=[win, hT], W=[ps])
                        S.copy('act', dst[:, h, :tw], ps[:, :tw], R=[ps], W=[dst])
                for h in range(4):
                    for jj in range(2):
                        ps = S.psum()
                        c0 = O_Q + (2 * h + jj) * 128
                        for k in range(8):
                            S.mm(ps[:, :tw], win[:, k, c0:c0 + 128], hT[:, k, :tw], k == 0, k == 7, R=[win, hT], W=[ps])
                        S.act(rs[:, jj, h, :tw], ps[:, :tw], AF.Silu, R=[ps], W=[rs])
                if nxt is not None:
                    self.norm_mod(nxt[0], nxt[1][1], 0, nxt[1][2], hTs[(n_ + 1) % 2], w['sq'], w['rstd'], w['tmp'])
                nck = tw // 128
                psks = []
                for cc in range(nck):
                    c0 = cc * 128
                    gc = g['c'][cc]
                    qk4 = qk4s[cc]
                    psks.append(self.gla_tokmajor(g, gc, win, hT, c0, 7 - cc))
                    for n2, d in enumerate('fb'):
                        ps = S.psum()
                        for h in range(4):
                            S.mm(ps[:, h * 128:(h + 1) * 128], gc['l' + d][:, h * 128:(h + 1) * 128], cst[:, n2, :], True, True,
                                 R=[gc['l' + d], cst], W=[ps])
                        psv = ps[:].rearrange('p (h t) -> p h t', h=4)
                        S.act(E4[2 * n2][:], psv, AF.Exp, R=[ps], W=[E4[2 * n2]], bias=LNQ)
                        S.act(E4[2 * n2 + 1][:], psv, AF.Exp, R=[ps], W=[E4[2 * n2 + 1]], scale=-1.0)
                        S.tt('dve', qk4[2 * n2][:], qTs[:, :, c0:c0 + 128], E4[2 * n2][:], ALU.mult, R=[qTs, E4[2 * n2]], W=[qk4[2 * n2]])
                        S.tt('pool', qk4[2 * n2 + 1][:], kTs[:, :, c0:c0 + 128], E4[2 * n2 + 1][:], ALU.mult, R=[kTs, E4[2 * n2 + 1]], W=[qk4[2 * n2 + 1]])
                for cc in range(nck):
                    n = (off + cc * 128) // 128
                    c0 = cc * 128
                    gc, psk, qk4 = g['c'][cc], psks[cc], qk4s[cc]
                    pA = []
                    for n2 in range(2):
                        ps = S.psum()
                        for h in range(4):
                            S.mm(ps[:, h * 128:(h + 1) * 128], qk4[2 * n2 + 1][:, h, :], qk4[2 * n2][:, h, :], True, True,
                                 R=[qk4[2 * n2], qk4[2 * n2 + 1]], W=[ps])
                        pA.append(ps)
                    S.tt('dve', A1[:], pA[0][:].rearrange('p (h t) -> p h t', h=4), mk4[:, 0], ALU.mult, R=[pA[0], mk4], W=[A1])
                    S.tt('dve', A2[:], pA[1][:].rearrange('p (h t) -> p h t', h=4), mk4[:, 1], ALU.mult, R=[pA[1], mk4], W=[A2])
                    S.tt('pool', Ab[:], A1[:], A2[:], ALU.add, R=[A1, A2], W=[Ab])
                    S.dma('sp', Sbl[:], SBs[:, n], W=[Sbl])
                    for h in range(4):
                        if chain == 0:
                            S.stt('dve', Sbe[:, h, :], Sin[:, 1, h, :], dcum[:, n, h:h + 1], Sbl[:, h, :], ALU.mult, ALU.add,
                                  R=[Sin, dcum, Sbl], W=[Sbe])
                        else:
                            S.copy('dve', Sbe[:, h, :], Sbl[:, h, :], R=[Sbl], W=[Sbe])
                    po = []
                    for jj in range(2):
                        ps = S.psum()
                        for h in range(4):
                            o_ = ps[:, h * 128:(h + 1) * 128]
                            S.mm(o_, gc['vbf'][:, h * 256 + jj * 128:h * 256 + (jj + 1) * 128], Ab[:, h, :], True, False, R=[gc['vbf'], Ab], W=[ps])
                            S.mm(o_, Sfb[:, h, jj * 128:(jj + 1) * 128], qk4[0][:, h, :], False, False, R=[Sfb, qk4[0]], W=[ps])
                            S.mm(o_, Sbe[:, h, jj * 128:(jj + 1) * 128], qk4[2][:, h, :], False, True, R=[Sbe, qk4[2]], W=[ps])
                        po.append(ps)
                        S.act(osq[jj][:], ps[:], AF.Square, R=[ps], W=[osq[jj]])
                    pss = S.psum()
                    S.mm(pss[:], self.ones[:], osq[0][:], True, False, R=[self.ones, osq[0]], W=[pss])
                    S.mm(pss[:], self.ones[:], osq[1][:], False, True, R=[self.ones, osq[1]], W=[pss])
                    S.act(rso[:], pss[:].rearrange('p (h t) -> p h t', h=4), AF.Sqrt, R=[pss], W=[rso], scale=1.0 / 256, bias=EPS)
                    S.recip(rso[:], rso[:], R=[rso], W=[rso])
                    for jj in range(2):
                        S.stt('dve', mtmp[:].rearrange('p h t -> p (h t)'), po[jj][:], self.vecs[:, vb + jj:vb + jj + 1],
                              rso[:].rearrange('p h t -> p (h t)'), ALU.mult, ALU.mult, R=[po[jj], self.vecs, rso], W=[mtmp])
                        S.tt('pool', mixT[:, jj, :, c0:c0 + 128], mtmp[:], rs[:, jj, :, c0:c0 + 128], ALU.mult, R=[mtmp, rs], W=[mixT])
                    bk = self.gla_dS(g, gc, psk, 'f')
                    for h in range(4):
                        S.stt('dve', Sf[:, h, :], Sf[:, h, :], gc['dec'][:, h:h + 1], bk[h // 2][:, (h % 2) * 256:(h % 2 + 1) * 256],
                              ALU.mult, ALU.add, R=[Sf, gc['dec'], bk[h // 2]], W=[Sf])
                    S.copy('pool', Sfb[:], Sf[:], R=[Sf], W=[Sfb])
                for o in range(8):
                    ps = S.psum()
                    for kc in range(8):
                        S.mm(ps[:, :tw], wo[:, kc, o * 128:(o + 1) * 128], mixT[:, kc % 2, kc // 2, :tw], kc == 0, kc == 7, R=[wo, mixT], W=[ps])
                    S.stt('dve', xt[:, o, :tw], ps[:, :tw], self.mod[:, 16 + o, j:j + 1], xt[:, o, :tw], ALU.mult, ALU.add,
                          R=[ps, self.mod, xt], W=[xt])
                S.dma('sp', xs[:, :, off:off + tw], xt[:, :, :tw], R=[xt])
        S.barrier()
        S.rr_n = 8
        S.release(m)

from concourse.bass_utils import run_bass_kernel_spmd

ROPE_PERM = np.array(list(range(8, 16)) + list(range(0, 8)) + list(range(24, 32)) + list(range(16, 24)))
ROPE_SIGN = np.array([-1.0] * 8 + [1.0] * 8 + [-1.0] * 8 + [1.0] * 8, np.float32)


def fm(v):
    return np.ascontiguousarray(np.asarray(v, np.float32).reshape(-1, 128).T)


def rope_tables(pos):
    r = (pos // 64).astype(np.float32)
    col = (pos % 64).astype(np.float32)
    inv = (np.float32(10000.0) ** (-np.arange(0, 16, 2, dtype=np.float32) / np.float32(16))).astype(np.float32)
    ar = (r[:, None] * inv).astype(np.float32)
    ac = (col[:, None] * inv).astype(np.float32)
    ang = np.concatenate([ar, ar, ac, ac], axis=-1)
    return np.cos(ang).astype(np.float32), np.sin(ang).astype(np.float32)


def prep_stores(inp, Lc, G=4):
    NT = Lc + NCTX
    x = np.asarray(inp['x'], np.float32)
    ctx = np.asarray(inp['ctx'], np.float32)
    shared = {}
    for i in range(DEPTH):
        shared['ada_w%d' % i] = np.ascontiguousarray(inp['ada_w'][i], np.float32)
        shared['mlp_w1_%d' % i] = np.ascontiguousarray(inp['mlp_w1'][i], np.float32)
        shared['mlp_w2_%d' % i] = np.ascontiguousarray(inp['mlp_w2'][i], np.float32)
    for j in range(2):
        w_in = np.asarray(inp['ev_w_in'][j], np.float32)
        shared['ev_w_in%d' % j] = np.ascontiguousarray(w_in)
        shared['ev_w_in_kp%d' % j] = np.ascontiguousarray(w_in[:, E_KV + ROPE_PERM])
        wuq = np.asarray(inp['ev_w_uq'][j], np.float32)
        shared['ev_w_uq%d' % j] = np.ascontiguousarray(wuq)
        wp = wuq.reshape(384, 8, 96).copy()
        wp[:, :, 64:] = wuq.reshape(384, 8, 96)[:, :, 64 + ROPE_PERM]
        shared['ev_w_uq_p%d' % j] = np.ascontiguousarray(wp.reshape(384, 768))
        shared['ev_w_ukv%d' % j] = np.ascontiguousarray(inp['ev_w_ukv'][j], np.float32)
        shared['ev_wsT%d' % j] = np.ascontiguousarray(np.transpose(np.asarray(inp['ev_cm_ws'][j], np.float32), (2, 0, 1)))
        bc = np.zeros((128, 640), np.float32)
        bc[:, 0:128] = np.asarray(inp['ev_cm_norm'][j], np.float32)[None, :]
        bc[:, 128:640] = np.asarray(inp['ev_cm_bs'][j], np.float32).reshape(1, 512)
        shared['ev_bc%d' % j] = bc
        shared['ev_w_out%d' % j] = np.ascontiguousarray(inp['ev_w_out'][j], np.float32)
        shared['od_w_in%d' % j] = np.ascontiguousarray(inp['od_w_in'][j], np.float32)
        shared['od_w_out%d' % j] = np.ascontiguousarray(inp['od_w_out'][j], np.float32)
        for d, wn, bn in (('f', 'od_w_gf', 'od_b_gf'), ('b', 'od_w_gb', 'od_b_gb')):
            a = np.zeros((17, 512), np.float32)
            a[0:16] = np.asarray(inp[wn][j], np.float32)
            a[16] = np.asarray(inp[bn][j], np.float32)
            shared['od_wg%s%d' % (d, j)] = a
        ob = np.zeros((128, 256), np.float32)
        ob[:, :] = np.asarray(inp['od_o_norm'][j], np.float32)[None, :]
        shared['od_bc%d' % j] = ob
    s_, t_ = np.meshgrid(np.arange(128), np.arange(128), indexing='ij')
    gc = np.zeros((128, 6, 128), np.float32)
    gc[:, 0] = np.where(s_ <= t_, -1.0 / 16, 0.0)
    gc[:, 1] = np.where(s_ >= t_, -1.0 / 16, 0.0)
    gc[:, 2] = np.where(s_ > t_, -1.0 / 16, 0.0)
    gc[:, 3] = np.where(s_ < t_, -1.0 / 16, 0.0)
    gc[:, 4] = np.where(s_ <= t_, 1.0, 0.0)
    gc[:, 5] = np.where(s_ >= t_, 1.0, 0.0)
    shared['gla_cst'] = gc
    stores = []
    for core in range(8):
        b, jc = core // 4, (core % 4 if G > 1 else 0)
        st = dict(shared)
        xs = np.concatenate([x[b, jc * Lc:(jc + 1) * Lc, :].T, ctx[b].T], axis=1)
        st['xs'] = np.ascontiguousarray(xs)
        vecs = np.zeros((128, NV), np.float32)
        for i in range(DEPTH):
            o = i * LV
            vecs[:, o:o + 48] = fm(inp['ada_b'][i])
            vecs[:, o + 48:o + 56] = fm(inp['norm1_g'][i])
            vecs[:, o + 56:o + 64] = fm(inp['norm2_g'][i])
            if i % 2 == 0:
                vecs[:, o + 64:o + 67] = fm(inp['ev_q_norm'][i // 2])
                vecs[:, o + 67:o + 69] = fm(inp['ev_kv_norm'][i // 2])
            else:
                vecs[:, o + 64:o + 66] = fm(inp['od_o_norm'][i // 2])
        g = DEPTH * LV
        vecs[:, g:g + 8] = fm(inp['c'][b])
        vecs[:, g + 8:g + 16] = fm(inp['c_ctx'])
        vecs[:, g + 16:g + 24] = fm(inp['final_g'])
        st['vecs'] = vecs
        cos, sin = rope_tables(np.arange(jc * Lc, (jc + 1) * Lc))
        cos = np.concatenate([cos, np.ones((NCTX, 32), np.float32)], 0).T
        sin = np.concatenate([sin * ROPE_SIGN[None, :], np.zeros((NCTX, 32), np.float32)], 0).T
        rq = np.zeros((96, 2, NT), np.float32)
        rq[0:64, 0, :] = MLA_SCALE
        rq[64:96, 0, :] = MLA_SCALE * cos
        rq[64:96, 1, :] = MLA_SCALE * sin
        st['ropeQ'] = rq
        rk = np.zeros((32, 2, NT), np.float32)
        rk[:, 0, :] = cos
        rk[:, 1, :] = sin
        st['ropeK'] = rk
        pm = np.zeros((128, 16), np.float32)
        for q in range(4):
            pm[:, q] = 1.0 if q < jc else 0.0
            pm[:, 4 + q] = 1.0 if q > jc else 0.0
        pm[:, 8:16] = 1.0 - pm[:, 0:8]
        st['posmask'] = pm
        stores.append(st)
    return stores


def build_program(phases, Lc, fused=False, G=4):
    nc = bass.Bass('TRN2', target_bir_lowering=False)
    b = B(nc, Lc, fused, G)
    S = b.S
    writes_xs = any(p[0] in ('E3', 'M', 'O2') for p in phases)
    if writes_xs:
        xin = b.env.r('xs_in', [D, b.NT], F32)
        xo = b.env.w('xs', [D, b.NT], F32)
        tk = Tok('xscopy')
        for k in range(8):
            S.dma('sp', xo[k * 128:(k + 1) * 128, :], xin[k * 128:(k + 1) * 128, :], W=[tk])
        S.barrier()
    for p in phases:
        getattr(b, 'adaln' if p[0] == 'adaln' else 'phase_' + p[0])(*p[1:])
    S.barrier()
    S.emit()
    return nc, b.env


def run_launch(phases, Lc, stores, trace=False):
    nc, env = build_program(phases, Lc, fused=False)
    in_maps = []
    for st in stores:
        im = {}
        for name in env.ins:
            src = 'xs' if name == 'xs_in' else name
            im[name] = st[src]
        in_maps.append(im)
    res = run_bass_kernel_spmd(nc, in_maps, core_ids=list(range(len(stores))), trace=trace)
    for st, r in zip(stores, res.results):
        for name in env.outs:
            st[name] = np.asarray(r[name])
    return res


def exchange_even(stores, Lc):
    for b in range(len(stores) // 4):
        grp = stores[b * 4:(b + 1) * 4]
        for name, allname in (('LATs', 'LATall'), ('KRs', 'KRall')):
            lat = np.concatenate([g[name][:, :Lc] for g in grp] + [grp[0][name][:, Lc:]], axis=1)
            for g in grp:
                g[allname] = lat


def exchange_odd(stores):
    for b in range(len(stores) // 4):
        grp = stores[b * 4:(b + 1) * 4]
        ex = np.ascontiguousarray(np.stack([g['EXs'] for g in grp], axis=1))
        for g in grp:
            g['EXall'] = ex


LAUNCHES = [
    ([('adaln', 0), ('E1', 0)], 'even'),
    ([('adaln', 0), ('E2', 0), ('E3', 0), ('M', 0, False), ('adaln', 1), ('O1', 1)], 'odd'),
    ([('adaln', 1), ('O2', 1), ('M', 1, False), ('adaln', 2), ('E1', 2)], 'even'),
    ([('adaln', 2), ('E2', 2), ('E3', 2), ('M', 2, False), ('adaln', 3), ('O1', 3)], 'odd'),
    ([('adaln', 3), ('O2', 3), ('M', 3, True)], None),
]


def kernel_unfused(**inputs):
    x = np.asarray(inputs['x'])
    Bn, L, _ = x.shape
    Lc = L // 4
    stores = prep_stores(inputs, Lc)
    for phases, ex in LAUNCHES:
        run_launch(phases, Lc, stores)
        if ex == 'even':
            exchange_even(stores, Lc)
        elif ex == 'odd':
            exchange_odd(stores)
    out = np.zeros((Bn, L, D), np.float32)
    for core in range(8):
        b, jc = core // 4, core % 4
        out[b, jc * Lc:(jc + 1) * Lc, :] = np.asarray(stores[core]['yT'], np.float32).T
    return out


FUSED_PHASES = [
    ('adaln', 0), ('E1', 0), ('E2', 0), ('E3', 0), ('M', 0, False),
    ('adaln', 1), ('O1', 1), ('O2', 1), ('M', 1, False),
    ('adaln', 2), ('E1', 2), ('E2', 2), ('E3', 2), ('M', 2, False),
    ('adaln', 3), ('O1', 3), ('O2', 3), ('M', 3, True),
]


def kernel(**inputs):
    x = np.asarray(inputs['x'])
    Bn, L, _ = x.shape
    stores = prep_stores(inputs, L, G=1)
    nc, env = build_program(FUSED_PHASES, L, fused=True, G=1)
    use = [b * 4 for b in range(Bn)]
    in_maps = [{name: stores[c]['xs' if name == 'xs_in' else name] for name in env.ins} for c in use]
    res = run_bass_kernel_spmd(nc, in_maps, core_ids=list(range(len(use))))
    out = np.zeros((Bn, L, D), np.float32)
    for b in range(Bn):
        out[b] = np.asarray(res.results[b]['yT'], np.float32).T
    return out
```
